# Optimizing a Trainium2 kernel written in Bass

```python
import math
import jax, jax.numpy as jnp
from jax import lax
import numpy as np

D_MODEL = 1024
BATCH = 16
SEQ = 2048
DEPTH = 1

N_MEM = 256
HEAD_DIM = 64
A_HEADS = 8
A_WIDTH = A_HEADS * HEAD_DIM
IDX_HEADS = 4
IDX_DIM = 64
TOPK_MAX = 256
B_HEADS = 4
B_VDIM = 2 * HEAD_DIM
B_WIDTH = B_HEADS * B_VDIM
MIX_WIDTH = A_WIDTH + B_WIDTH
IN_SIZES = (A_WIDTH, HEAD_DIM, HEAD_DIM, IDX_HEADS * IDX_DIM, IDX_DIM, IDX_HEADS,
            2 * B_HEADS * HEAD_DIM, 2 * B_HEADS * HEAD_DIM, B_WIDTH)
IN_COLS = sum(IN_SIZES)
X_HEADS = 4
X_HEAD_DIM = D_MODEL // X_HEADS
D_FF = 2816
CONV_W = 3
ROPE_THETA = 10000.0
EPS = 1e-6
Q_BLOCK = 128

kernel_name = "hybrid_dsa_diffattn_memxattn_convglu"


def rms_normalize(x):
    xf = x.astype(jnp.float32)
    y = xf * lax.rsqrt(jnp.mean(xf * xf, axis=-1, keepdims=True) + EPS)
    return y.astype(x.dtype)


def rmsnorm(x, g):
    xf = x.astype(jnp.float32)
    y = xf * lax.rsqrt(jnp.mean(xf * xf, axis=-1, keepdims=True) + EPS)
    return (y * g.astype(jnp.float32)).astype(x.dtype)


def rope_tables(positions, dim):
    inv_freq = 1.0 / (ROPE_THETA ** (jnp.arange(0, dim, 2, dtype=jnp.float32) / dim))
    ang = positions.astype(jnp.float32)[..., None] * inv_freq
    return jnp.cos(ang), jnp.sin(ang)


def rope(x, cos, sin):
    shape = cos.shape[:2] + (1,) * (x.ndim - 3) + cos.shape[-1:]
    c = cos.reshape(shape).astype(x.dtype)
    s = sin.reshape(shape).astype(x.dtype)
    x1, x2 = jnp.split(x, 2, axis=-1)
    return jnp.concatenate([x1 * c - x2 * s, x2 * c + x1 * s], axis=-1)


def to_blocks(t, n_blk):
    return t.reshape((t.shape[0], n_blk, Q_BLOCK) + t.shape[2:]).swapaxes(0, 1)


def from_blocks(t):
    t = t.swapaxes(0, 1)
    return t.reshape((t.shape[0], t.shape[1] * t.shape[2]) + t.shape[3:])


def dsa_attention(q, k, v, q_idx, k_idx, w_idx):
    B, S, H, D = q.shape
    n_blk = S // Q_BLOCK
    topk = min(TOPK_MAX, S // 4)
    key_pos = jnp.arange(S)

    def block(args):
        qb, qib, wib, start = args
        qpos = start + jnp.arange(Q_BLOCK)
        causal = key_pos[None, :] <= qpos[:, None]
        logits = jnp.einsum('bqhd,bsd->bqhs', qib, k_idx)
        score = jnp.einsum('bqh,bqhs->bqs', wib, jax.nn.relu(logits)).astype(jnp.float32)
        score = jnp.where(causal[None], score, -jnp.inf)
        _, idx = lax.top_k(score, topk)
        k_sel = jax.vmap(lambda kk, ii: kk[ii])(k, idx)
        v_sel = jax.vmap(lambda vv, ii: vv[ii])(v, idx)
        valid = idx <= qpos[None, :, None]
        s = jnp.einsum('bqhd,bqkd->bqhk', qb, k_sel).astype(jnp.float32) * (D ** -0.5)
        s = jnp.where(valid[:, :, None, :], s, -jnp.inf)
        p = jax.nn.softmax(s, axis=-1).astype(v.dtype)
        return jnp.einsum('bqhk,bqkd->bqhd', p, v_sel)

    starts = jnp.arange(n_blk) * Q_BLOCK
    out = lax.map(block, (to_blocks(q, n_blk), to_blocks(q_idx, n_blk), to_blocks(w_idx, n_blk), starts))
    return from_blocks(out)


def diff_attention(q, k, v, lam):
    B, S, H, _, D = q.shape
    n_blk = S // Q_BLOCK
    key_pos = jnp.arange(S)

    def block(args):
        qb, start = args
        qpos = start + jnp.arange(Q_BLOCK)
        causal = key_pos[None, :] <= qpos[:, None]
        s = jnp.einsum('bqhcd,bshcd->bhcqs', qb, k).astype(jnp.float32) * (D ** -0.5)
        s = jnp.where(causal, s, -jnp.inf)
        p = jax.nn.softmax(s, axis=-1)
        a = (p[:, :, 0] - lam * p[:, :, 1]).astype(v.dtype)
        return jnp.einsum('bhqs,bshe->bqhe', a, v)

    starts = jnp.arange(n_blk) * Q_BLOCK
    out = lax.map(block, (to_blocks(q, n_blk), starts))
    return from_blocks(out)


def token_mixer(hn, cos, sin, w_in, g_qa, g_ka, g_qb, g_kb,
                lam_q1, lam_k1, lam_q2, lam_k2, w_out, lambda_init):
    B, S, _ = hn.shape
    proj = hn @ w_in
    offsets = np.cumsum(IN_SIZES)[:-1].tolist()
    q_a, k_a, v_a, q_i, k_i, w_i, q_b, k_b, v_b = jnp.split(proj, offsets, axis=-1)

    q_a = rope(rmsnorm(q_a.reshape(B, S, A_HEADS, HEAD_DIM), g_qa), cos, sin)
    k_a = rope(rmsnorm(k_a, g_ka), cos, sin)
    q_i = rope(q_i.reshape(B, S, IDX_HEADS, IDX_DIM), cos, sin)
    k_i = rope(k_i, cos, sin)
    w_i = w_i * (IDX_HEADS ** -0.5 * IDX_DIM ** -0.5)
    out_a = dsa_attention(q_a, k_a, v_a, q_i, k_i, w_i).reshape(B, S, A_WIDTH)

    q_b = rope(rmsnorm(q_b.reshape(B, S, B_HEADS, 2, HEAD_DIM), g_qb), cos, sin)
    k_b = rope(rmsnorm(k_b.reshape(B, S, B_HEADS, 2, HEAD_DIM), g_kb), cos, sin)
    v_b = v_b.reshape(B, S, B_HEADS, B_VDIM)
    f32 = jnp.float32
    lam = (jnp.exp(jnp.sum(lam_q1.astype(f32) * lam_k1.astype(f32)))
           - jnp.exp(jnp.sum(lam_q2.astype(f32) * lam_k2.astype(f32))) + lambda_init)
    out_b = diff_attention(q_b, k_b, v_b, lam)
    out_b = (rms_normalize(out_b) * (1.0 - lambda_init)).reshape(B, S, B_WIDTH)

    return jnp.concatenate([out_a, out_b], axis=-1) @ w_out


def cross_attention(hn, memn, w_xq, w_xk, w_xv, w_xo, g_xq, g_xk):
    B, S, _ = hn.shape
    M = memn.shape[1]
    q = rmsnorm((hn @ w_xq).reshape(B, S, X_HEADS, X_HEAD_DIM), g_xq)
    k = rmsnorm((memn @ w_xk).reshape(B, M, X_HEADS, X_HEAD_DIM), g_xk)
    v = (memn @ w_xv).reshape(B, M, X_HEADS, X_HEAD_DIM)
    s = jnp.einsum('bshd,bmhd->bhsm', q, k).astype(jnp.float32) * (X_HEAD_DIM ** -0.5)
    p = jax.nn.softmax(s, axis=-1).astype(v.dtype)
    o = jnp.einsum('bhsm,bmhd->bshd', p, v).reshape(B, S, D_MODEL)
    return o @ w_xo


def conv_glu(hn, w_ffn_in, conv_w, conv_b, w_ffn_out):
    S = hn.shape[1]
    a, gate = jnp.split(hn @ w_ffn_in, 2, axis=-1)
    a_pad = jnp.pad(a, ((0, 0), (CONV_W - 1, 0), (0, 0)))
    conv = conv_b + sum(a_pad[:, j:j + S] * conv_w[j] for j in range(CONV_W))
    return (jax.nn.gelu(conv) * gate) @ w_ffn_out


def setup_inputs(seed: int = 0) -> dict:
    key = jax.random.key(seed)
    ks = iter(jax.random.split(key, 48))
    f32 = jnp.float32

    def nrm(shape, fan_in):
        return jax.random.normal(next(ks), shape, f32) * (fan_in ** -0.5)

    def gain(shape):
        return 1.0 + 0.02 * jax.random.normal(next(ks), shape, f32)

    x = jax.random.normal(next(ks), (BATCH, SEQ, D_MODEL), f32)
    mem = jax.random.normal(next(ks), (BATCH, N_MEM, D_MODEL), f32)
    offs = jax.random.randint(next(ks), (BATCH, 1), 0, 1024)
    positions = (offs + jnp.arange(SEQ, dtype=jnp.int32)[None, :]).astype(jnp.int32)
    L = DEPTH
    return {
        "x": x,
        "mem": mem,
        "positions": positions,
        "g_mix": gain((L, D_MODEL)),
        "w_in": nrm((L, D_MODEL, IN_COLS), D_MODEL),
        "g_qa": gain((L, HEAD_DIM)),
        "g_ka": gain((L, HEAD_DIM)),
        "g_qb": gain((L, HEAD_DIM)),
        "g_kb": gain((L, HEAD_DIM)),
        "lam_q1": 0.1 * jax.random.normal(next(ks), (L, HEAD_DIM), f32),
        "lam_k1": 0.1 * jax.random.normal(next(ks), (L, HEAD_DIM), f32),
        "lam_q2": 0.1 * jax.random.normal(next(ks), (L, HEAD_DIM), f32),
        "lam_k2": 0.1 * jax.random.normal(next(ks), (L, HEAD_DIM), f32),
        "w_out": nrm((L, MIX_WIDTH, D_MODEL), MIX_WIDTH),
        "g_xattn": gain((L, D_MODEL)),
        "g_mem": gain((L, D_MODEL)),
        "w_xq": nrm((L, D_MODEL, D_MODEL), D_MODEL),
        "w_xk": nrm((L, D_MODEL, D_MODEL), D_MODEL),
        "w_xv": nrm((L, D_MODEL, D_MODEL), D_MODEL),
        "w_xo": nrm((L, D_MODEL, D_MODEL), D_MODEL),
        "g_xq": gain((L, X_HEAD_DIM)),
        "g_xk": gain((L, X_HEAD_DIM)),
        "g_ffn": gain((L, D_MODEL)),
        "w_ffn_in": nrm((L, D_MODEL, 2 * D_FF), D_MODEL),
        "conv_w": nrm((L, CONV_W, D_FF), CONV_W),
        "conv_b": 0.02 * jax.random.normal(next(ks), (L, D_FF), f32),
        "w_ffn_out": nrm((L, D_FF, D_MODEL), D_FF),
    }


def reference(x, mem, positions, g_mix, w_in, g_qa, g_ka, g_qb, g_kb,
              lam_q1, lam_k1, lam_q2, lam_k2, w_out, g_xattn, g_mem,
              w_xq, w_xk, w_xv, w_xo, g_xq, g_xk, g_ffn, w_ffn_in,
              conv_w, conv_b, w_ffn_out):
    cos, sin = rope_tables(positions, HEAD_DIM)
    h = x
    for l in range(DEPTH):
        lambda_init = 0.8 - 0.6 * math.exp(-0.3 * l)
        h = h + token_mixer(rmsnorm(h, g_mix[l]), cos, sin, w_in[l], g_qa[l], g_ka[l],
                            g_qb[l], g_kb[l], lam_q1[l], lam_k1[l], lam_q2[l], lam_k2[l],
                            w_out[l], lambda_init)
        h = h + cross_attention(rmsnorm(h, g_xattn[l]), rmsnorm(mem, g_mem[l]),
                                w_xq[l], w_xk[l], w_xv[l], w_xo[l], g_xq[l], g_xk[l])
        h = h + conv_glu(rmsnorm(h, g_ffn[l]), w_ffn_in[l], conv_w[l], conv_b[l], w_ffn_out[l])
    return h
```

```python
import numpy as np
from contextlib import ExitStack
import concourse.bass as bass
import concourse.mybir as mybir
from concourse.bass_utils import run_bass_kernel_spmd

F32 = mybir.dt.float32; BF16 = mybir.dt.bfloat16; I32 = mybir.dt.int32
AF = mybir.ActivationFunctionType
ALU = mybir.AluOpType
AX = mybir.AxisListType

D = 1024; DC = 8; NMEM = 256; DFF = 2816; NF = 22
NCOL = 2628
EPS = 1e-6
NBIS = 13
TWO_PI = 6.283185307179586


class _Op:
    __slots__ = ('eng', 'fn', 'deps', 'is_dma', 'dslot', 'dval', 'idx', 'signal', 'sigval')


class Prog:
    ENGS = ['pe', 'act', 'dve', 'pool', 'sp']

    def __init__(self, nc, n_dma=12):
        self.nc = nc
        self.ops = {e: [] for e in self.ENGS}
        self.last_w = {}
        self.readers = {}
        self.seen = {e: {} for e in self.ENGS}
        self.n_dma = n_dma
        self.dma_rr = {e: 0 for e in self.ENGS}
        self.dma_last = {}

    def op(self, eng, fn, reads=(), writes=(), dma=False):
        o = _Op(); o.eng = eng; o.fn = fn; o.is_dma = dma; o.signal = False; o.sigval = 0
        o.idx = len(self.ops[eng]); o.dslot = None; o.dval = 0
        deps = []
        for r in reads:
            w = self.last_w.get(r)
            if w is not None: deps.append(w)
        for r in writes:
            w = self.last_w.get(r)
            if w is not None: deps.append(w)
            rd = self.readers.get(r)
            if rd: deps.extend(rd.values())
        if dma:
            k = self.dma_rr[eng]; self.dma_rr[eng] = (k + 1) % self.n_dma
            o.dslot = (eng, k)
            prev = self.dma_last.get(o.dslot)
            if prev is not None:
                deps.append(prev); o.dval = prev.dval + 16
            else:
                o.dval = 16
            self.dma_last[o.dslot] = o
        seen = self.seen[eng]
        final = []
        for d in deps:
            if d.is_dma:
                key = d.dslot
                if seen.get(key, 0) >= d.dval: continue
                seen[key] = d.dval
                final.append(d)
            else:
                if d.eng == eng and eng not in ('act', 'pool', 'dve'): continue
                key = d.eng
                if seen.get(key, -1) >= d.idx: continue
                seen[key] = d.idx
                d.signal = True
                final.append(d)
        o.deps = final
        for r in reads:
            self.readers.setdefault(r, {})[o.dslot if dma else eng] = o
        for r in writes:
            self.last_w[r] = o
            self.readers[r] = {}
        self.ops[eng].append(o)
        return o

    def pe(self, fn, reads=(), writes=()): return self.op('pe', fn, reads, writes)
    def act(self, fn, reads=(), writes=()): return self.op('act', fn, reads, writes)
    def dve(self, fn, reads=(), writes=()): return self.op('dve', fn, reads, writes)
    def pool(self, fn, reads=(), writes=()): return self.op('pool', fn, reads, writes)
    def dma(self, fn, reads=(), writes=(), q='sp'): return self.op(q, fn, reads, writes, dma=True)

    def alias(self, old_names, new_names):
        best = {}
        def add(o):
            if o is None: return
            if o.is_dma: best[('d', o.dslot, o.dval)] = o
            else:
                p = best.get(('e', o.eng))
                if p is None or o.idx > p.idx: best[('e', o.eng)] = o
        for n in old_names:
            add(self.last_w.get(n))
            for r in self.readers.get(n, {}).values(): add(r)
        for n in old_names:
            self.last_w.pop(n, None); self.readers.pop(n, None)
        for n in new_names:
            self.last_w[n] = None
            self.readers[n] = dict(best)

    def emit(self):
        nc = self.nc
        for e in self.ENGS:
            c = 0
            for o in self.ops[e]:
                if o.signal and not o.is_dma:
                    c += 1; o.sigval = c
        with ExitStack() as st:
            esem = {e: st.enter_context(nc.semaphore("sem_" + e)) for e in self.ENGS}
            dsem = {}
            for slot in self.dma_last:
                dsem[slot] = st.enter_context(nc.semaphore("dsem_%s_%d" % slot))
            block = st.enter_context(nc.Block())

            def run(ename, e):
                for o in self.ops[ename]:
                    for d in o.deps:
                        if d.is_dma: e.wait_ge(dsem[d.dslot], d.dval)
                        else: e.wait_ge(esem[d.eng], d.sigval)
                    ins = o.fn(e)
                    if o.is_dma: ins.then_inc(dsem[o.dslot], 16)
                    elif o.signal: ins.then_inc(esem[ename], 1)
                if ename == 'sp':
                    for slot, o in self.dma_last.items():
                        e.wait_ge(dsem[slot], o.dval)

            @block.tensor
            def _(e): run('pe', e)
            @block.scalar
            def _(e): run('act', e)
            @block.vector
            def _(e): run('dve', e)
            @block.gpsimd
            def _(e): run('pool', e)
            @block.sync
            def _(e): run('sp', e)


def build(S, n_seq=2, topk=None, stop_after=None, dbg=False):
    NT = S // 128
    NQC = S // 512
    if topk is None: topk = min(256, S // 4)
    FC = min(1024, S)
    NFC = S // FC
    nc = bass.Bass("TRN2", target_bir_lowering=False)
    di = lambda n, s, d=F32: nc.dram_tensor(n, s, d, kind="ExternalInput").ap()
    x_d = di("x", [n_seq, S, D]); mem_d = di("mem", [n_seq, NMEM, D]); pos_d = di("pos", [n_seq, 128, NT], I32)
    invf_d = di("invf", [128, 32])
    gmix_d = di("gmix", [128, D]); gxat_d = di("gxat", [128, D]); gmem_d = di("gmem", [128, D]); gffn_d = di("gffn", [128, D])
    gn_d = di("gn", [128, 1664]); gxq_d = di("gxq", [128, D]); gxk_d = di("gxk", [128, D]); lam_d = di("lamv", [128, 256])
    win_d = di("w_in", [D, NCOL]); wout_d = di("w_out", [D, D]); wxq_d = di("w_xq", [D, D]); wxk_d = di("w_xk", [D, D])
    wxv_d = di("w_xv", [D, D]); wxo_d = di("w_xo", [D, D]); wfi_d = di("w_ffn_in", [D, 2 * DFF]); wfo_d = di("w_ffn_out", [DFF, D])
    cw_d = di("convw", [128, NF, 3]); cb_d = di("convb", [128, NF])
    out_d = nc.dram_tensor("out", [n_seq, S, D], F32, kind="ExternalOutput").ap()

    P = Prog(nc)
    st = ExitStack()
    _cnt = [0]
    def sb(shape, dt=F32, name=None):
        _cnt[0] += 1
        return st.enter_context(nc.sbuf_tensor("s_" + (name or ("t%d" % _cnt[0])), shape, dt))

    dbg_outs = {}
    def dump(name, ap, shape, dt, reads):
        if not dbg: return
        d = nc.dram_tensor("dbg_" + name, list(shape), dt, kind="ExternalOutput").ap()
        dbg_outs[name] = d
        P.dma(lambda e: e.dma_start(out=d, in_=ap), reads=reads, q='sp')

    Q = [st.enter_context(nc.psum_tensor("Q%d" % i, [128, 1024], F32)) for i in range(4)]
    BK = {}
    for i in range(4):
        BK['Q%da' % i] = Q[i][:, 0:512]; BK['Q%db' % i] = Q[i][:, 512:1024]
    QTv = {'Q3a': Q[3][:, 0:512].bitcast(BF16), 'Q3b': Q[3][:, 512:1024].bitcast(BF16), 'Q2a': Q[2][:, 0:512].bitcast(BF16)}
    rot = {}
    def rotbank(pool):
        k = rot.get(pool, 0); rot[pool] = k + 1
        return pool[k % len(pool)]
    MMB = ('Q0a', 'Q0b', 'Q1a', 'Q1b')
    TRB = ('Q3a', 'Q3b')

    NBIG = max(20 * S + 66 * NT, 8 * S + 3 * 2048, 22 * FC + 8 * FC)
    BIG = sb([128, NBIG], BF16, "BIG")
    ACT = sb([128, max(8 * S, 4 * S + 8192)], BF16, "ACTA")
    WS = [sb([128, 4096], BF16, "WS%d" % i) for i in range(4)]
    ws_rr = [0]
    def ws_next():
        k = ws_rr[0]; ws_rr[0] = (k + 1) % 4
        return k
    off = [0]
    def carve(n):
        a = BIG[:, off[0]:off[0] + n]; off[0] += n
        return a
    qaT = carve(4 * S).rearrange("p (c s) -> p c s", c=4)
    kaT = carve(S)
    qbT = carve(4 * S).rearrange("p (c s) -> p c s", c=4)
    kbT = carve(4 * S).rearrange("p (c s) -> p c s", c=4)
    qiT = carve(2 * S).rearrange("p (c s) -> p c s", c=2)
    kiT = carve(S)
    vb = carve(4 * S).rearrange("p (t c) -> p t c", t=NT)
    va = carve(66 * NT).rearrange("p (t c) -> p t c", t=NT)
    L1 = ['qaT', 'kaT', 'qbT', 'kbT', 'qiT', 'kiT', 'vb', 'va']
    qxT = BIG[:, 0:8 * S].rearrange("p (c s) -> p c s", c=8)
    kxT = BIG[:, 8 * S:8 * S + 2048].rearrange("p (c s) -> p c s", c=8)
    vx = BIG[:, 8 * S + 2048:8 * S + 4096].rearrange("p (t c) -> p t c", t=2)
    memT = BIG[:, 8 * S + 4096:8 * S + 6144].rearrange("p (c s) -> p c s", c=8)
    L2 = ['qxT', 'kxT', 'vx', 'memT']
    uT = BIG[:, 0:22 * FC].rearrange("p (f s) -> p f s", f=NF)
    hTc = BIG[:, 22 * FC:30 * FC].rearrange("p (c s) -> p c s", c=8)
    L3 = ['uT', 'hTc']
    xT = ACT[:, 0:8 * S].rearrange("p (c s) -> p c s", c=8)
    scoreF = ACT[:, 0:2 * S].bitcast(F32)
    maskq = ACT[:, 2 * S:3 * S]
    maskT = [ACT[:, 3 * S:4 * S].rearrange("p (k q) -> p k q", k=NT) for i in range(2)]
    relub = [ACT[:, 4 * S + i * 1024:4 * S + (i + 1) * 1024].bitcast(F32) for i in range(2)]
    mixA = ACT[0:64, 4 * S + 2048:4 * S + 6144].rearrange("p (h q) -> p h q", h=8)
    mixB = ACT[:, 4 * S + 6144:4 * S + 8192].rearrange("p (h q) -> p h q", h=4)
    LA = ['xT']
    LB = ['scoreF0', 'maskq0', 'maskT0', 'relub0', 'relub1', 'mixA', 'mixB']

    ident = sb([128, 128], BF16, "ident"); ones = sb([128, 128], BF16, "ones");
    negtri = sb([128, 128], F32, "negtri")
    pow2 = sb([128, NBIS], F32, "pow2"); mhalf = sb([128, 32], F32, "mhalf")
    invf = sb([128, 32], F32, "invf")
    Gt = sb([128, D], F32, "Gt"); gn = sb([128, 1664], F32, "gn")
    lamv = sb([128, 256], F32, "lamv"); lsc = sb([128, 8], F32, "lsc")
    cw = sb([128, NF, 3], F32, "cw"); cb = sb([128, NF], F32, "cb")
    posi = sb([128, NT], I32, "posi"); posf = sb([128, NT], F32, "posf")
    cosT = sb([128, NT, 32], F32, "cosT"); sinT = sb([128, NT, 32], F32, "sinT")
    rstdx = sb([128, NT], F32, "rstdx"); ssqx = sb([128, NT], F32, "ssqx")
    wi = sb([128, NT, 4], F32, "wi")
    xst = [sb([128, D], F32, "xst%d" % i) for i in range(2)]
    XNB = sb([128, 2 * D + 8], BF16, "XNB")
    xnb = [XNB[:, i * D:(i + 1) * D] for i in range(2)]
    sqj = sb([128, D], F32, "sqj")
    angt = xst[0][:, 0:NT * 32].rearrange("p (t c) -> p t c", c=32)
    angi = xst[1][:, 0:NT * 32].bitcast(I32).rearrange("p (t c) -> p t c", c=32)
    angk = sqj[:, 0:NT * 32].rearrange("p (t c) -> p t c", c=32)
    G2 = sb([128, D], F32, "G2")
    Pf = [G2[:, i * 512:(i + 1) * 512] for i in range(2)]
    Pn = sb([128, 512], F32, "Pn")
    Rb = [sb([128, 512], BF16, "Rb%d" % i) for i in range(2)]
    hs = sb([128, 64], F32, "hs")
    aS = sqj[:, 0:514]; cv = Pf[0]; gl = Pf[1]; t1 = Pn; t2 = Pf[0]; sqb = Rb[0]
    Eb = [sb([128, 1024], BF16, "Eb%d" % i) for i in range(2)]
    rs = sb([128, 1024], F32, "rs")
    rt = [rs[:, i * 256:(i + 1) * 256] for i in range(4)]
    bisS = [sb([128, 8], F32, "bis%d" % i) for i in range(2)]; wisS = [sb([128, NBIS], F32, "wis%d" % i) for i in range(2)]; wis2S = [sb([128, NBIS], F32, "wis2_%d" % i) for i in range(2)]
    halo = sb([128, NF, 2], F32, "halo")
    aSb = [XNB[:, i * (FC + 2):(i + 1) * (FC + 2)] for i in range(2)]
    dg = sb([128, 6, 128], BF16, "dg")
    epsT = sb([128, 1], F32, "epsT")
    onesf = sb([128, 128], F32, "onesf"); biasb = sb([128, 2, 512], BF16, "biasb"); tri01 = sb([128, 128], BF16, "tri01"); dbiasM = sb([128, 896], BF16, "dbiasM")

    def dma_in(out_ap, in_ap, w, q='sp', r=()):
        P.dma(lambda e: e.dma_start(out=out_ap, in_=in_ap), reads=r, writes=w, q=q)
    def dma_out(out_ap, in_ap, r, w=()):
        P.dma(lambda e: e.dma_start(out=out_ap, in_=in_ap), reads=r, writes=w, q='sp')
    def mm(out, lhsT, rhs, start, stop, r, w):
        P.pe(lambda e: e.matmul(out, lhsT=lhsT, rhs=rhs, start=start, stop=stop), reads=r, writes=w)
    def tr(out, in_, r, w):
        P.pe(lambda e: e.transpose(out=out, in_=in_, identity=ident[:]), reads=list(r) + ['ident'], writes=w)
    def act(out, in_, func, r, w, scale=None, bias=None, accum=None):
        kw = {}
        if scale is not None: kw['scale'] = scale
        if bias is not None: kw['bias'] = bias
        if accum is not None: kw['accum_out'] = accum
        P.act(lambda e: e.activation(out=out, in_=in_, func=func, **kw), reads=r, writes=w)
    def tt(out, in0, in1, op, r, w, eng='dve'):
        P.op(eng, lambda e: e.tensor_tensor(out=out, in0=in0, in1=in1, op=op), reads=r, writes=w)
    def ts(out, in0, s1, s2, op0, op1, r, w, accum=None, eng='dve'):
        if op1 is None:
            P.op(eng, lambda e: e.tensor_scalar(out=out, in0=in0, scalar1=s1, scalar2=None, op0=op0), reads=r, writes=w)
        elif accum is None:
            P.op(eng, lambda e: e.tensor_scalar(out=out, in0=in0, scalar1=s1, scalar2=s2, op0=op0, op1=op1), reads=r, writes=w)
        else:
            P.op(eng, lambda e: e.tensor_scalar(out=out, in0=in0, scalar1=s1, scalar2=s2, op0=op0, op1=op1, accum_out=accum), reads=r, writes=w)
    def stt(out, in0, scalar, in1, op0, op1, r, w):
        P.dve(lambda e: e.scalar_tensor_tensor(out=out, in0=in0, scalar=scalar, in1=in1, op0=op0, op1=op1), reads=r, writes=w)
    def red(out, in_, op, r, w):
        P.dve(lambda e: e.tensor_reduce(out=out, in_=in_, axis=AX.X, op=op), reads=r, writes=w)
    def cp(out, in_, r, w, eng='dve'):
        if eng == 'act':
            act(out, in_, AF.Copy, r, w)
        else:
            P.op(eng, lambda e: e.tensor_copy(out=out, in_=in_), reads=r, writes=w)
    def memset(ap, v, w, eng='dve'):
        P.op(eng, lambda e: e.memset(ap, v), writes=w)
    def recip(out, in_, r, w):
        P.dve(lambda e: e.reciprocal(out=out, in_=in_), reads=r, writes=w)
    def rstd_of(out, ssq, n, inv_dim, r, w):
        if n == 1:
            ts(out, ssq, inv_dim, EPS, ALU.mult, ALU.add, r, w)
            tt(out, out, mhalf[:, 0:n], ALU.pow, list(w) + ['mhalf'], w, eng='pool')
        else:
            act(out, ssq, AF.Ln, list(r) + ['epsT'], w, scale=inv_dim, bias=epsT[:, 0:1])
            act(out, out, AF.Exp, w, w, scale=-0.5)

    memset(epsT[:], EPS, ['epsT']); memset(onesf[:], 1.0, ['onesf']); memset(ones[:], 1.0, ['ones']); memset(mhalf[:], -0.5, ['mhalf'])
    identf = sqj[:, 0:128]
    memset(identf[:], 1.0, ['sqj'], eng='pool')
    P.pool(lambda e: e.affine_select(out=identf[:], in_=identf[:], pattern=[[-1, 128]], compare_op=ALU.is_equal,
                                     fill=0.0, base=0, channel_multiplier=1), reads=['sqj'], writes=['sqj'])
    cp(ident[:], identf[:], ['sqj'], ['ident'])
    memset(negtri[:], 0.0, ['negtri'], eng='pool')
    P.pool(lambda e: e.affine_select(out=negtri[:], in_=negtri[:], pattern=[[-1, 128]], compare_op=ALU.is_ge,
                                     fill=-1e30, base=0, channel_multiplier=1), reads=['negtri'], writes=['negtri'])
    memset(tri01[:], 1.0, ['tri01'], eng='pool')
    P.pool(lambda e: e.affine_select(out=tri01[:], in_=tri01[:], pattern=[[1, 128]], compare_op=ALU.is_ge,
                                     fill=0.0, base=0, channel_multiplier=-1), reads=['tri01'], writes=['tri01'])
    memset(dbiasM[:], 0.0, ['dbias'], eng='pool')
    P.pool(lambda e: e.affine_select(out=dbiasM[:], in_=dbiasM[:], pattern=[[1, 896]], compare_op=ALU.is_ge,
                                     fill=-30000.0, base=-384, channel_multiplier=-1), reads=['dbias'], writes=['dbias'])
    for i in range(NBIS):
        memset(pow2[:, i:i + 1], 2.0 ** -(i + 1), ['pow2'])
    dma_in(invf[:], invf_d[:, :], ['invf']); dma_in(gn[:], gn_d[:, :], ['gn']); dma_in(lamv[:], lam_d[:, :], ['lamv'])
    dma_in(cw[:], cw_d[:, :, :], ['cw']); dma_in(cb[:], cb_d[:, :], ['cb'])
    tt(sqj[:, 0:64], lamv[:, 0:64], lamv[:, 64:128], ALU.mult, ['lamv'], ['sqj'])
    tt(sqj[:, 64:128], lamv[:, 128:192], lamv[:, 192:256], ALU.mult, ['lamv'], ['sqj'])
    red(lsc[:, 0:2], sqj[:, 0:128].rearrange("p (a b) -> p a b", a=2), ALU.add, ['sqj'], ['lsc'])
    act(lsc[:, 2:4], lsc[:, 0:2], AF.Exp, ['lsc'], ['lsc'])
    tt(lsc[:, 4:5], lsc[:, 3:4], lsc[:, 2:3], ALU.subtract, ['lsc'], ['lsc'])
    ts(lsc[:, 5:6], lsc[:, 4:5], -0.2, None, ALU.add, None, ['lsc'], ['lsc'])
    neglam = lsc[:, 5:6]

    tile_rr = [0]
    def norm_transpose(src_rows_ap, Gname, Gtile, dstT, dst_cols, dst_res, ssq_out=None, rstd_out=None, extra_r=(), split=False):
        k = tile_rr[0] % 2; tile_rr[0] += 1
        xs_, xn_ = xst[k], xnb[k]
        dma_in(xs_[:], src_rows_ap, ['xst%d' % k], r=extra_r)
        ssq = ssq_out if ssq_out is not None else hs[:, 60:61]
        rstd = rstd_out if rstd_out is not None else hs[:, 61:62]
        act(sqj[:], xs_[:], AF.Square, ['xst%d' % k], ['sqj', 'st_ssq'], accum=ssq)
        rstd_of(rstd, ssq, 1, 1.0 / D, ['st_ssq'], ['st_rstd'])
        stt(xn_, xs_[:], rstd, Gtile[:], ALU.mult, ALU.mult, ['xst%d' % k, 'st_rstd', Gname], ['xnb%d' % k])
        if split:
            return k
        norm_post(k, dstT, dst_cols, dst_res)
        return k

    def norm_post(k, dstT, dst_cols, dst_res):
        xn_ = xnb[k]
        b = rotbank(TRB)
        for c in range(DC):
            tr(QTv[b][:, c * 128:(c + 1) * 128], xn_[:, c * 128:(c + 1) * 128], ['xnb%d' % k], [b])
        cp(dstT[:, :, dst_cols], QTv[b][:, :].rearrange("p (c s) -> p c s", c=8), [b], [dst_res], eng='act')

    def load_w(dst_ap, src_ap, res):
        dma_in(dst_ap, src_ap, [res], q='pool')

    for sq in range(n_seq):
        ws_rr[0] = 0
        P.alias(LB, LA)
        P.alias(['rsA', 'rsD'], ['rt0', 'rt1', 'rt2', 'rt3'])
        dma_in(Gt[:], gmix_d[:, :], ['Gt'])
        dma_in(posi[:], pos_d[sq, :, :], ['posi'])
        cp(posf[:], posi[:], ['posi'], ['posf'])
        tt(angt, posf[:].unsqueeze(2).broadcast_to([128, NT, 32]), invf[:].unsqueeze(1).broadcast_to([128, NT, 32]),
           ALU.mult, ['posf', 'invf'], ['xst0'])
        for which, dst in ((0, sinT), (1, cosT)):
            ts(angk, angt, 1.0 / TWO_PI, 0.5 + 0.25 * which, ALU.mult, ALU.add, ['xst0'], ['sqj'])
            cp(angi, angk, ['sqj'], ['xst1'])
            cp(angk, angi, ['xst1'], ['sqj'])
            stt(angk, angk, -6.28125, angt, ALU.mult, ALU.add, ['sqj', 'xst0'], ['sqj'])
            cp(dst[:], angi, ['xst1'], ['rope_tmp'])
            stt(angk, dst[:], -(TWO_PI - 6.28125), angk, ALU.mult, ALU.add, ['rope_tmp', 'sqj'], ['sqj'])
            if which:
                ts(angk, angk, np.pi / 2, None, ALU.add, None, ['sqj'], ['sqj'])
            ts(dst[:], angk, np.pi, -TWO_PI, ALU.is_gt, ALU.mult, ['sqj'], ['rope_tmp'])
            tt(angk, angk, dst[:], ALU.add, ['sqj', 'rope_tmp'], ['sqj'])
            ts(dst[:], angk, -np.pi, TWO_PI, ALU.is_lt, ALU.mult, ['sqj'], ['rope_tmp'])
            tt(angk, angk, dst[:], ALU.add, ['sqj', 'rope_tmp'], ['sqj'])
            act(dst[:], angk, AF.Sin, ['sqj', 'rope_tmp'], ['sinT' if which == 0 else 'cosT'])
        for t in range(NT):
            norm_transpose(x_d[sq, t * 128:(t + 1) * 128, :], 'Gt', Gt, xT, slice(t * 128, (t + 1) * 128), 'xT',
                           ssq_out=ssqx[:, t:t + 1], rstd_out=rstdx[:, t:t + 1])
        memset(va[:, :, 64:65], 1.0, ['va'])

        groups = [(0, 512, 'qa'), (512, 196, 'ka'), (708, 512, 'qb'), (1220, 512, 'kb'), (1732, 384, 'qi'), (2116, 512, 'vb')]
        gn_off = {'qa': 0, 'ka': 512, 'qb': 640, 'kb': 1152}
        gw = {}
        def load_group(gi):
            c0, ncg, kind = groups[gi]
            wsl = ws_next(); wres = 'ws%d' % wsl
            wv = WS[wsl][:, 0:8 * ncg].rearrange("p (k c) -> p k c", k=8)
            load_w(wv, win_d.rearrange("(k p) c -> p k c", p=128)[:, :, c0:c0 + ncg], wres)
            gw[gi] = (wv, wres)

        def p1_mm(gi, t):
            c0, ncg, kind = groups[gi]
            wv, wres = gw[gi]
            tok = slice(t * 128, (t + 1) * 128)
            b = rotbank(MMB); ps = BK[b][:, 0:ncg]
            for k in range(DC):
                mm(ps, xT[:, k, tok], wv[:, k, :], k == 0, k == DC - 1, ['xT', wres], [b])
            return b

        def p1_post(gi, t, b):
            c0, ncg, kind = groups[gi]
            tok = slice(t * 128, (t + 1) * 128)
            ps = BK[b][:, 0:ncg]
            if kind == 'vb':
                act(vb[:, t, :], ps, AF.Copy, [b], ['vb'])
                return
            nrope = {'qa': 512, 'ka': 128, 'qb': 512, 'kb': 512, 'qi': 384}[kind]
            nh = nrope // 64
            pf = Pf[t % 2]; pres = 'Pf%d' % (t % 2)
            act(pf[:, 0:ncg], ps, AF.Copy, [b], [pres])
            src = pf
            if kind == 'ka':
                cp(va[:, t, 0:64], pf[:, 128:192], [pres], ['va'], eng='pool')
                ts(wi[:, t, :], pf[:, 192:196], 1.0 / 16.0, None, ALU.mult, None, [pres], ['wi'], eng='pool')
            if kind != 'qi':
                hsl = hs[:, (t % 2) * 8:(t % 2) * 8 + nh]; hres = 'hs%d' % (t % 2)
                act(sqj[:, 0:nrope], BK[b][:, 0:nrope], AF.Square, [b], ['sqj'])
                red(hsl, sqj[:, 0:nrope].rearrange("p (h d) -> p h d", d=64), ALU.add, ['sqj'], [hres])
                g0 = gn_off[kind]
                tt(Pn[:, 0:nrope], pf[:, 0:nrope], gn[:, g0:g0 + nrope], ALU.mult, [pres, 'gn'], ['Pn'])
                act(hsl, hsl, AF.Ln, [hres, 'epsT'], [hres], scale=1.0 / 64, bias=epsT[:, 0:1])
                act(hsl, hsl, AF.Exp, [hres], [hres], scale=-0.5)
                tt(Pn[:, 0:nrope].rearrange("p (h d) -> p h d", d=64), Pn[:, 0:nrope].rearrange("p (h d) -> p h d", d=64),
                   hsl.unsqueeze(2).broadcast_to([128, nh, 64]), ALU.mult, ['Pn', hres], ['Pn'])
                src = Pn; pres = 'Pn'
            sv = src[:, 0:nrope].rearrange("p (h d) -> p h d", d=64)
            x1 = sv[:, :, 0:32]; x2 = sv[:, :, 32:64]
            cb_ = cosT[:, t, :].unsqueeze(1).broadcast_to([128, nh, 32])
            sb_ = sinT[:, t, :].unsqueeze(1).broadcast_to([128, nh, 32])
            rv = [rt[i][:, 0:nh * 32].rearrange("p (h d) -> p h d", d=32) for i in range(4)]
            R = Rb[t % 2]; rres = 'Rb%d' % (t % 2)
            Rv = R[:, 0:nrope].rearrange("p (h d) -> p h d", d=64)
            tt(rv[0], x1, cb_, ALU.mult, [pres, 'cosT'], ['rt0'])
            tt(rv[2], x2, cb_, ALU.mult, [pres, 'cosT'], ['rt2'], eng='pool')
            tt(rv[1], x2, sb_, ALU.mult, [pres, 'sinT'], ['rt1'])
            tt(rv[3], x1, sb_, ALU.mult, [pres, 'sinT'], ['rt3'], eng='pool')
            tt(Rv[:, :, 0:32], rv[0], rv[1], ALU.subtract, ['rt0', 'rt1'], [rres + 'lo'])
            tt(Rv[:, :, 32:64], rv[2], rv[3], ALU.add, ['rt2', 'rt3'], [rres + 'hi'], eng='pool')
            nb = nrope // 128
            tb = rotbank(TRB)
            for j in range(nb):
                tr(QTv[tb][:, j * 128:(j + 1) * 128], R[:, j * 128:(j + 1) * 128], [rres + 'lo', rres + 'hi'], [tb])
            tv = QTv[tb][:, 0:nb * 128].rearrange("p (c s) -> p c s", c=nb)
            if kind == 'qa': cp(qaT[:, :, tok], tv, [tb], ['qaT'], eng='act')
            elif kind == 'ka': cp(kaT[:, tok], QTv[tb][:, 0:128], [tb], ['kaT'], eng='act')
            elif kind == 'qb': cp(qbT[:, :, tok], tv, [tb], ['qbT'], eng='act')
            elif kind == 'kb': cp(kbT[:, :, tok], tv, [tb], ['kbT'], eng='act')
            else:
                cp(qiT[:, :, tok], tv[:, 0:2, :], [tb], ['qiT'], eng='act')
                cp(kiT[:, tok], QTv[tb][:, 256:384], [tb], ['kiT'], eng='act')

        load_group(0)
        pend = None
        for gi in range(len(groups)):
            if gi + 1 < len(groups): load_group(gi + 1)
            for t in range(NT):
                b = p1_mm(gi, t)
                if pend is not None: p1_post(*pend)
                pend = (gi, t, b)
        p1_post(*pend)

        if sq == 0:
            dump("xT", xT, [128, 8, S], BF16, ['xT']); dump("cosT", cosT[:], [128, NT, 32], F32, ['cosT']); dump("sinT", sinT[:], [128, NT, 32], F32, ['sinT'])
            dump("qaT", qaT, [128, 4, S], BF16, ['qaT']); dump("kaT", kaT, [128, S], BF16, ['kaT']); dump("qbT", qbT, [128, 4, S], BF16, ['qbT'])
            dump("kbT", kbT, [128, 4, S], BF16, ['kbT']); dump("qiT", qiT, [128, 2, S], BF16, ['qiT']); dump("kiT", kiT, [128, S], BF16, ['kiT'])
            dump("vb", vb, [128, NT, 512], BF16, ['vb']); dump("va", va, [128, NT, 66], BF16, ['va']); dump("wi", wi[:], [128, NT, 4], F32, ['wi'])
            dump("rstdx", rstdx[:], [128, NT], F32, ['st_rstd'])
            dump("hs", hs[:], [128, 64], F32, ['hs0', 'hs1']); dump("sqj", sqj[:], [128, D], F32, ['sqj']); dump("Pn", Pn[:], [128, 512], F32, ['Pn'])
            dump("Pf1", Pf[(NT - 1) % 2][:], [128, 512], F32, ['Pf%d' % ((NT - 1) % 2)])
        if stop_after == 'P1':
            break
        P.alias(LA, LB)
        P.alias(['rt0', 'rt1', 'rt2', 'rt3'], ['rsA', 'rsD'])
        P.alias(['ws1'], ['scoreF1']); P.alias(['xnb0', 'xnb1'], ['maskq1']); P.alias(['sqj'], ['sqjlo', 'ED1'])
        scoreFs = [scoreF, WS[1][:, :].bitcast(F32)]; maskqs = [maskq, XNB[:, 0:S]]
        woA = [None, None]; woAr = [None, None]
        for hf in range(2):
            k = ws_next(); woAr[hf] = 'ws%d' % k
            woA[hf] = WS[k][0:64, :].rearrange("p (h c) -> p h c", h=8)
            load_w(woA[hf], wout_d[0:512, hf * 512:(hf + 1) * 512].rearrange("(h d) c -> d h c", d=64), woAr[hf])
        k = ws_next(); woBr = 'ws%d' % k
        woB = WS[k][:, :].rearrange("p (h c) -> p h c", h=4)
        load_w(woB, wout_d[512:1024, :].rearrange("(h p) c -> p h c", p=128), woBr)

        mixA4 = mixA.rearrange("p (h two) q -> p h two q", two=2)
        rsA = rs[0:64, 0:512]; rsD = rs[:, 512:1024]
        rrow = rs[64:65, 0:512]

        def gen_bisect(qt):
            k_ = qt % 2
            scF = scoreFs[k_]; sres = 'scoreF%d' % k_; mq = maskqs[k_]; mqres = 'maskq%d' % k_
            bis = bisS[k_]; bres = 'bis%d' % k_; wis = wisS[k_]; wis2 = wis2S[k_]; wres = 'wis%d' % k_
            lo = bis[:, 1:2]; mid = bis[:, 3:4]; cnt = bis[:, 4:5]; tmp = bis[:, 5:6]; thr = bis[:, 6:7]; nmid = bis[:, 7:8]
            qtok = slice(qt * 128, (qt + 1) * 128)
            nk = (qt + 1) * 128
            nkc = (nk + 511) // 512
            for kc in range(nkc):
                k0 = kc * 512; kn = min(512, nk - k0)
                for hp2 in range(2):
                    for half in range(2):
                        hp = slice(half * 64, half * 64 + 64); lb = ('Q2a', 'Q2b')[half]
                        mm(BK[lb][:, 0:kn], qiT[hp, hp2, qtok], kiT[hp, k0:k0 + kn], True, True, ['qiT', 'kiT'], [lb])
                    for half in range(2):
                        h = 2 * hp2 + half; lb = ('Q2a', 'Q2b')[half]
                        rb_ = relub[half]; rr_ = 'relub%d' % half
                        act(rb_[:, 0:kn], BK[lb][:, 0:kn], AF.Relu, [lb], [rr_])
                        if h == 0:
                            ts(scF[:, k0:k0 + kn], rb_[:, 0:kn], wi[:, qt, 0:1], None, ALU.mult, None, [rr_, 'wi'], [sres])
                        else:
                            stt(scF[:, k0:k0 + kn], rb_[:, 0:kn], wi[:, qt, h:h + 1], scF[:, k0:k0 + kn], ALU.mult, ALU.add,
                                [rr_, 'wi', sres], [sres])
                    yield
            red(bis[:, 0:1], scF[:, 0:nk], ALU.max, [sres], [bres])
            red(bis[:, 1:2], scF[:, 0:nk], ALU.min, [sres], [bres])
            yield
            tt(scF[:, nk - 128:nk], scF[:, nk - 128:nk], negtri[:], ALU.add, [sres, 'negtri'], [sres])
            tt(bis[:, 2:3], bis[:, 0:1], bis[:, 1:2], ALU.subtract, [bres], [bres])
            ts(wis[:], pow2[:], bis[:, 2:3], None, ALU.mult, None, ['pow2', bres], [wres])
            ts(wis2[:], pow2[:], bis[:, 2:3], 2.0, ALU.mult, ALU.mult, ['pow2', bres], [wres])
            tt(mid, lo, wis[:, 0:1], ALU.add, [bres, wres], [bres])
            ts(nmid, mid, -1.0, None, ALU.mult, None, [bres], [bres])
            yield
            thrS = float(2 * topk - nk - 1)
            for i in range(NBIS):
                act(mq[:, 0:nk], scF[:, 0:nk], AF.Sign, [sres, bres], [mqres, bres], bias=nmid, accum=cnt)
                if i < NBIS - 1:
                    ts(tmp, cnt, thrS, wis2[:, i + 1:i + 2], ALU.is_ge, ALU.mult, [bres, wres], [bres])
                    stt(nmid, nmid, wis[:, i + 1:i + 2], tmp, ALU.add, ALU.subtract, [bres, wres], [bres])
                else:
                    ts(tmp, cnt, thrS, wis[:, i:i + 1], ALU.is_ge, ALU.mult, [bres, wres], [bres])
                    stt(thr, nmid, wis[:, i:i + 1], tmp, ALU.add, ALU.subtract, [bres, wres], [bres])
                yield
            ts(mq[:, 0:nk], scF[:, 0:nk], thr, 0.0, ALU.add, ALU.is_ge, [sres, bres], [mqres])
            yield

        def mask_transposes(qt):
            k_ = qt % 2
            mq = maskqs[k_]; mqres = 'maskq%d' % k_
            mT = maskT[0]
            for g0 in range(0, qt + 1, 8):
                g1 = min(qt + 1, g0 + 8)
                for kb in range(g0, g1):
                    tr(QTv['Q2a'][:, (kb - g0) * 128:(kb - g0 + 1) * 128], mq[:, kb * 128:(kb + 1) * 128], [mqres], ['Q2a'])
                cp(mT[:, g0:g1, :], QTv['Q2a'][:, 0:(g1 - g0) * 128].rearrange("p (k q) -> p k q", q=128), ['Q2a'], ['maskT0'], eng='act')
            if dbg and sq == 0 and qt == 1:
                dump("scoreF", scoreFs[k_][:, 0:S], [128, S], F32, ['scoreF%d' % k_]); dump("bis", bisS[k_][:], [128, 8], F32, ['bis%d' % k_])
                dump("maskq", mq, [128, S], BF16, [mqres]); dump("maskT", mT, [128, NT, 128], BF16, ['maskT0'])

        def gen_attn(qt):
            qtok = slice(qt * 128, (qt + 1) * 128)
            qq = qt % 4
            pure = (qt + 1) * 128 <= topk
            mT = maskT[0]
            def stage1(kb):
                ks = slice(kb * 128, (kb + 1) * 128)
                mm(BK['Q0a'], kaT[0:64, ks], qaT[0:64, :, qtok], True, True, ['kaT', 'qaT'], ['Q0a'])
                mm(BK['Q0b'], kaT[64:128, ks], qaT[64:128, :, qtok], True, True, ['kaT', 'qaT'], ['Q0b'])
                E = Eb[kb % 2]; eres = 'Eb%d' % (kb % 2)
                act(E[:], Q[0][:, :], AF.Exp, ['Q0a', 'Q0b'], [eres], scale=0.125)
                if not (pure and kb < qt):
                    if pure:
                        mk = tri01[:].unsqueeze(1).broadcast_to([128, 8, 128]); mr = 'tri01'
                    else:
                        mk = mT[:, kb, :].unsqueeze(1).broadcast_to([128, 8, 128]); mr = 'maskT0'
                    tt(E[:].rearrange("p (h q) -> p h q", h=8), E[:].rearrange("p (h q) -> p h q", h=8), mk, ALU.mult, [eres, mr], [eres])
            def stage2(kb):
                E = Eb[kb % 2]; eres = 'Eb%d' % (kb % 2)
                mm(BK['Q1a'][0:65, :], va[:, kb, 0:65], E[:, 0:512], kb == 0, kb == qt, ['va', eres], ['Q1a'])
                mm(BK['Q1b'][0:65, :], va[:, kb, 0:65], E[:, 512:1024], kb == 0, kb == qt, ['va', eres], ['Q1b'])
            prev = None
            for kb in range(qt + 1):
                stage1(kb)
                if prev is not None: stage2(prev)
                prev = kb
                yield
            stage2(prev)
            yield
            for par in range(2):
                ob = ('Q1a', 'Q1b')[par]; fb = ('Q0a', 'Q0b')[par]
                act(rrow, BK[ob][64:65, :], AF.Ln, [ob], ['rrow'])
                act(rrow, rrow, AF.Exp, ['rrow'], ['rrow'], scale=-1.0)
                P.pe(lambda e, fb=fb: e.matmul(BK[fb][0:64, :], lhsT=onesf[64:65, 0:64], rhs=rrow, start=True, stop=True),
                     reads=['onesf', 'rrow'], writes=[fb])
                cp(rsA, BK[fb][0:64, :], [fb], ['rsA'], eng='act')
                tt(mixA4[:, :, par, qq * 128:(qq + 1) * 128], BK[ob][0:64, :].rearrange("p (h q) -> p h q", h=4),
                   rsA.rearrange("p (h q) -> p h q", h=4), ALU.mult, [ob, 'rsA'], ['mixA'])
                yield

        def gen_diff(qc):
            qs = slice(qc * 512, (qc + 1) * 512)
            nkb = 4 * qc + 4
            A = [Pf[1], sqj[:, 0:512]]; Ares = ['Pf1', 'sqjlo']; Aeng = ['dve', 'pool']
            EDb = [biasb[:].rearrange("p a b -> p (a b)"), sqj[:, 512:1024].bitcast(BF16)]
            for h in range(4):
                def stage1(kb):
                    ks = slice(kb * 128, (kb + 1) * 128)
                    diag = kb >= 4 * qc
                    for c in range(2):
                        cp_ = slice(c * 64, c * 64 + 64); sb_ = ('Q2a', 'Q2b')[c]
                        mm(BK[sb_], kbT[cp_, h, ks], qbT[cp_, h, qs], True, not diag, ['kbT', 'qbT'], [sb_])
                    if diag:
                        for c in range(2):
                            sb_ = ('Q2a', 'Q2b')[c]
                            mm(BK[sb_], ident[:], dbiasM[:, 384 - 128 * (kb - 4 * qc):896 - 128 * (kb - 4 * qc)], False, True, ['ident', 'dbias'], [sb_])
                    act(EDb[kb % 2], Q[2][:, :], AF.Exp, ['Q2a', 'Q2b'], ['ED%d' % (kb % 2)], scale=0.125)
                def stage2(kb):
                    E = EDb[kb % 2]; eres = 'ED%d' % (kb % 2)
                    for c in range(2):
                        ob = ('Q3a', 'Q3b')[c]
                        Ec = E[:, c * 512:(c + 1) * 512]
                        mm(BK[ob], vb[:, kb, h * 128:(h + 1) * 128], Ec, kb == 0, kb == nkb - 1, ['vb', eres], [ob])
                        if kb == 0: cp(A[c], Ec, [eres], [Ares[c]], eng=Aeng[c])
                        else: tt(A[c], A[c], Ec, ALU.add, [eres, Ares[c]], [Ares[c]], eng=Aeng[c])
                prev = None
                for kb in range(nkb):
                    stage1(kb)
                    if prev is not None: stage2(prev)
                    prev = kb
                    yield
                stage2(prev)
                yield
                for c in range(2):
                    zb = ('Q2a', 'Q2b')[c]; ob = ('Q3a', 'Q3b')[c]
                    mm(BK[zb], onesf[:, :], A[c], True, True, ['onesf', Ares[c]], [zb])
                    act(rsD, BK[zb], AF.Ln, [zb], ['rsD'])
                    act(rsD, rsD, AF.Exp, ['rsD'], ['rsD'], scale=-1.0)
                    tt((t1 if c == 0 else t2)[:, :], BK[ob], rsD, ALU.mult, [ob, 'rsD'], ['Pn' if c == 0 else 'Pf0'])
                    yield
                stt(t1[:], t2[:], neglam, t1[:], ALU.mult, ALU.add, ['Pn', 'Pf0', 'lsc'], ['Pn'])
                tt(sqb[:], t1[:], t1[:], ALU.mult, ['Pn'], ['Rb0'])
                mm(BK['Q2a'], ones[:], sqb[:], True, True, ['ones', 'Rb0'], ['Q2a'])
                act(rsD, BK['Q2a'], AF.Ln, ['Q2a', 'epsT'], ['rsD'], scale=1.0 / 128, bias=epsT[:, 0:1])
                act(rsD, rsD, AF.Exp, ['rsD'], ['rsD'], scale=-0.5)
                stt(mixB[:, h, :], t1[:], 0.8, rsD, ALU.mult, ALU.mult, ['Pn', 'rsD'], ['mixB'])
                yield

        def emit_wout(qc):
            if dbg and sq == 0 and qc == 0:
                dump("mixA", mixA, [64, 8, 512], BF16, ['mixA']); dump("mixB", mixB, [128, 4, 512], BF16, ['mixB'])
            for tt_ in range(4):
                t = qc * 4 + tt_
                ml = slice(tt_ * 128, (tt_ + 1) * 128)
                k2 = tile_rr[0] % 2; tile_rr[0] += 1
                dma_in(xst[k2][:], x_d[sq, t * 128:(t + 1) * 128, :], ['xst%d' % k2])
                for hf in range(2):
                    b = rotbank(('Q0a', 'Q0b'))
                    for hh in range(8):
                        mm(BK[b], mixA[:, hh, ml], woA[hf][:, hh, :], hh == 0, False, ['mixA', woAr[hf]], [b])
                    for hh in range(4):
                        mm(BK[b], mixB[:, hh, ml], woB[:, hh, hf * 512:(hf + 1) * 512], False, hh == 3, ['mixB', woBr], [b])
                    tt(xst[k2][:, hf * 512:(hf + 1) * 512], xst[k2][:, hf * 512:(hf + 1) * 512], BK[b], ALU.add, ['xst%d' % k2, b], ['xst%d' % k2])
                dma_out(out_d[sq, t * 128:(t + 1) * 128, :], xst[k2][:], ['xst%d' % k2], w=['out_%d_%d' % (sq, t)])

        def run_rr(fg, bg=()):
            fg = [g for g in fg if g is not None]; bg = [g for g in bg if g is not None]
            while fg:
                for g in list(fg):
                    try: next(g)
                    except StopIteration: fg.remove(g)
                for g in list(bg):
                    try: next(g)
                    except StopIteration: bg.remove(g)

        dgen = None
        older = None
        def bisected(t): return t < NT and (t + 1) * 128 > topk
        for s_ in range(NT + 1):
            fg = []
            if older is not None: fg.append(older)
            if s_ >= 1: fg.append(gen_attn(s_ - 1))
            younger = gen_bisect(s_ + 1) if bisected(s_ + 1) else None
            if s_ >= 1 and (s_ - 1) % 4 == 0:
                dgen = gen_diff((s_ - 1) // 4)
            last_of_chunk = s_ >= 1 and (s_ - 1) % 4 == 3
            if last_of_chunk:
                fg.append(dgen); run_rr(fg, [younger])
            else:
                run_rr(fg, [dgen, younger])
            if bisected(s_): mask_transposes(s_)
            if last_of_chunk:
                emit_wout((s_ - 1) // 4); dgen = None
            older = younger

        if stop_after == 'P2':
            continue
        P.alias(L1, L2 + ['hTc'])
        P.alias(['scoreF1'], ['ws1']); P.alias(['maskq1'], ['xnb0', 'xnb1']); P.alias(['sqjlo', 'ED1'], ['sqj'])
        P.alias(['Pf0', 'Pf1'], ['G2'])
        dma_in(Gt[:], gmem_d[:, :], ['Gt'])
        for mt in range(2):
            norm_transpose(mem_d[sq, mt * 128:(mt + 1) * 128, :], 'Gt', Gt, memT, slice(mt * 128, (mt + 1) * 128), 'memT')
        dma_in(G2[:], gxk_d[:, :], ['G2'])
        def wpieces(w_d):
            res = []
            for hf in range(2):
                k = ws_next()
                v = WS[k][:, :].rearrange("p (k c) -> p k c", k=8)
                load_w(v, w_d.rearrange("(k p) c -> p k c", p=128)[:, :, hf * 512:(hf + 1) * 512], 'ws%d' % k)
                res.append((v, 'ws%d' % k))
            return res

        def proj_headnorm(srcT, src_res, tok, wp, Gname, Gtile, dstT, dst_res, dst_cols):
            k2 = tile_rr[0] % 2; tile_rr[0] += 1
            xs_ = xst[k2]; xr = 'xst%d' % k2
            for hf in range(2):
                b = rotbank(MMB)
                for k in range(DC):
                    mm(BK[b], srcT[:, k, tok], wp[hf][0][:, k, :], k == 0, k == DC - 1, [src_res, wp[hf][1]], [b])
                for hh in range(2):
                    h4 = hf * 2 + hh
                    act(xs_[:, h4 * 256:(h4 + 1) * 256], BK[b][:, hh * 256:(hh + 1) * 256], AF.Copy, [b], [xr])
                    act(sqj[:, h4 * 256:(h4 + 1) * 256], BK[b][:, hh * 256:(hh + 1) * 256], AF.Square, [b], ['sqj', 'hs'], accum=hs[:, 32 + h4:33 + h4])
            rstd_of(hs[:, 32:36], hs[:, 32:36], 4, 1.0 / 256, ['hs'], ['hs'])
            for h4 in range(4):
                stt(xnb[k2][:, h4 * 256:(h4 + 1) * 256], xs_[:, h4 * 256:(h4 + 1) * 256], hs[:, 32 + h4:33 + h4], Gtile[:, h4 * 256:(h4 + 1) * 256],
                    ALU.mult, ALU.mult, [xr, 'hs', Gname], ['xnb%d' % k2])
            tb = rotbank(TRB)
            for c in range(DC):
                tr(QTv[tb][:, c * 128:(c + 1) * 128], xnb[k2][:, c * 128:(c + 1) * 128], ['xnb%d' % k2], [tb])
            cp(dstT[:, :, dst_cols], QTv[tb][:, :].rearrange("p (c s) -> p c s", c=8), [tb], [dst_res], eng='act')

        wk = wpieces(wxk_d)
        for mt in range(2):
            proj_headnorm(memT, 'memT', slice(mt * 128, (mt + 1) * 128), wk, 'G2', G2, kxT, 'kxT', slice(mt * 128, (mt + 1) * 128))
        wv_ = wpieces(wxv_d)
        for mt in range(2):
            for hf in range(2):
                b = rotbank(MMB)
                for k in range(DC):
                    mm(BK[b], memT[:, k, mt * 128:(mt + 1) * 128], wv_[hf][0][:, k, :], k == 0, k == DC - 1, ['memT', wv_[hf][1]], [b])
                cp(vx[:, mt, hf * 512:(hf + 1) * 512], BK[b], [b], ['vx'], eng='act')
        dma_in(Gt[:], gxat_d[:, :], ['Gt'])
        dma_in(G2[:], gxq_d[:, :], ['G2'])
        wq = wpieces(wxq_d)
        P.alias(LB, LA)
        def q_stageB(t):
            tok = slice(t * 128, (t + 1) * 128)
            hsl = hs[:, 32 + 4 * (t % 2):36 + 4 * (t % 2)]; hres = 'hsq%d' % (t % 2)
            banks = []
            for hf in range(2):
                b = rotbank(MMB); banks.append(b)
                for k in range(DC):
                    mm(BK[b], xT[:, k, tok], wq[hf][0][:, k, :], k == 0, k == DC - 1, ['xT', wq[hf][1]], [b])
                for hh in range(2):
                    h4 = hf * 2 + hh
                    act(sqj[:, h4 * 256:(h4 + 1) * 256], BK[b][:, hh * 256:(hh + 1) * 256], AF.Square, [b], ['sqj', hres], accum=hsl[:, h4:h4 + 1])
            return (t, banks)
        def q_stageC(t, banks):
            tok = slice(t * 128, (t + 1) * 128)
            k2 = tile_rr[0] % 2; tile_rr[0] += 1
            hsl = hs[:, 32 + 4 * (t % 2):36 + 4 * (t % 2)]; hres = 'hsq%d' % (t % 2)
            act(hsl, hsl, AF.Ln, [hres, 'epsT'], [hres], scale=1.0 / 256, bias=epsT[:, 0:1])
            act(hsl, hsl, AF.Exp, [hres], [hres], scale=-0.5)
            for h4 in range(4):
                b = banks[h4 // 2]; hh = h4 % 2
                stt(xnb[k2][:, h4 * 256:(h4 + 1) * 256], BK[b][:, hh * 256:(hh + 1) * 256], hsl[:, h4:h4 + 1], G2[:, h4 * 256:(h4 + 1) * 256],
                    ALU.mult, ALU.mult, [b, hres, 'G2'], ['xnb%d' % k2])
            tb = rotbank(TRB)
            for c in range(DC):
                tr(QTv[tb][:, c * 128:(c + 1) * 128], xnb[k2][:, c * 128:(c + 1) * 128], ['xnb%d' % k2], [tb])
            cp(qxT[:, :, tok], QTv[tb][:, :].rearrange("p (c s) -> p c s", c=8), [tb], ['qxT'], eng='act')
        pB = None
        for t in range(NT + 2):
            if t < NT:
                norm_transpose(out_d[sq, t * 128:(t + 1) * 128, :], 'Gt', Gt, xT, slice(t * 128, (t + 1) * 128), 'xT', extra_r=['out_%d_%d' % (sq, t)])
            pC = pB
            pB = q_stageB(t - 1) if 1 <= t <= NT else None
            if pC is not None: q_stageC(*pC)

        wo = wpieces(wxo_d)
        P.alias(LA, ['mixX'])
        dma_in(Gt[:], gffn_d[:, :], ['Gt'])
        mixX = xT
        def x_head_mm(qc, h):
            qs = slice(qc * 512, (qc + 1) * 512)
            hp_ = h % 2
            obs = (('Q1a', 'Q1b'), ('Q2a', 'Q2b'))[hp_]; zb = ('Q3a', 'Q3b')[hp_]
            for mt in range(2):
                b = ('Q0a', 'Q0b')[mt]
                for hh in range(2):
                    mm(BK[b], kxT[:, 2 * h + hh, mt * 128:(mt + 1) * 128], qxT[:, 2 * h + hh, qs], hh == 0, hh == 1, ['kxT', 'qxT'], [b])
                E = Eb[mt][:, 0:512]; eres = 'Eb%d' % mt
                act(E, BK[b], AF.Exp, [b], [eres], scale=1.0 / 16)
                for hh in range(2):
                    mm(BK[obs[hh]], vx[:, mt, (2 * h + hh) * 128:(2 * h + hh + 1) * 128], E, mt == 0, mt == 1, ['vx', eres], [obs[hh]])
                mm(BK[zb], ones[:], E, mt == 0, mt == 1, ['ones', eres], [zb])
        def x_head_fin(qc, h):
            qs = slice(qc * 512, (qc + 1) * 512)
            hp_ = h % 2
            obs = (('Q1a', 'Q1b'), ('Q2a', 'Q2b'))[hp_]; zb = ('Q3a', 'Q3b')[hp_]
            rz = rs[:, hp_ * 512:(hp_ + 1) * 512]; rzr = ('rsA', 'rsD')[hp_]
            act(rz, BK[zb], AF.Ln, [zb], [rzr])
            act(rz, rz, AF.Exp, [rzr], [rzr], scale=-1.0)
            for hh in range(2):
                tt(mixX[:, 2 * h + hh, qs], BK[obs[hh]], rz, ALU.mult, [obs[hh], rzr], ['mixX'])
        def x_wo(t):
            tok = slice(t * 128, (t + 1) * 128)
            k2 = tile_rr[0] % 2; tile_rr[0] += 1
            dma_in(xst[k2][:], out_d[sq, t * 128:(t + 1) * 128, :], ['xst%d' % k2], r=['out_%d_%d' % (sq, t)])
            for hf in range(2):
                b = rotbank(('Q0a', 'Q0b'))
                for k in range(DC):
                    mm(BK[b], mixX[:, k, tok], wo[hf][0][:, k, :], k == 0, k == DC - 1, ['mixX', wo[hf][1]], [b])
                tt(xst[k2][:, hf * 512:(hf + 1) * 512], xst[k2][:, hf * 512:(hf + 1) * 512], BK[b], ALU.add, ['xst%d' % k2, b], ['xst%d' % k2])
            dma_out(out_d[sq, t * 128:(t + 1) * 128, :], xst[k2][:], ['xst%d' % k2], w=['out_%d_%d' % (sq, t)])
        for qc in range(NQC):
            prevh = None
            for h in range(4):
                x_head_mm(qc, h)
                if prevh is not None: x_head_fin(qc, prevh)
                prevh = h
            x_head_fin(qc, prevh)
            for tt_ in range(4):
                t = qc * 4 + tt_
                x_wo(t)
                if t < FC // 128:
                    norm_transpose(out_d[sq, t * 128:(t + 1) * 128, :], 'Gt', Gt, hTc, slice(t * 128, (t + 1) * 128), 'hTc',
                                   extra_r=['out_%d_%d' % (sq, t)])

        if stop_after == 'P4':
            continue
        P.alias(L2, ['uT'])
        P.alias(['G2'], ['Pf0', 'Pf1'])
        P.alias(['mixX'], LB)
        memset(halo[:], 0.0, ['halo'])
        ntl = FC // 128
        FFB = ('Q0a', 'Q0b', 'Q1a', 'Q1b', 'Q2a', 'Q2b')
        ACCS = (('Q0a', 'Q0b', 'Q1a', 'Q1b'), ('Q2a', 'Q2b', 'Q3a', 'Q3b'))
        it_n = [0]; grp_n = [0]
        wsrc = wfi_d.rearrange("(k p) c -> p k c", p=128)

        def ffn_in_piece(fc, p0):
            npf = min(4, NF - p0)
            st_ = {}
            def load():
                ka_ = ws_next(); kg_ = ws_next()
                st_['ka'] = ka_; st_['kg'] = kg_
                st_['wa'] = WS[ka_][:, 0:8 * npf * 128].rearrange("p (k c) -> p k c", k=8)
                st_['wg'] = WS[kg_][:, 0:8 * npf * 128].rearrange("p (k c) -> p k c", k=8)
                load_w(st_['wa'], wsrc[:, :, p0 * 128:(p0 + npf) * 128], 'ws%d' % ka_)
                load_w(st_['wg'], wsrc[:, :, DFF + p0 * 128:DFF + (p0 + npf) * 128], 'ws%d' % kg_)
            def compute():
                ka_, kg_, wa, wg = st_['ka'], st_['kg'], st_['wa'], st_['wg']
                for fl in range(npf):
                    f = p0 + fl
                    par = f % 2
                    aSf = aSb[par]; ares = 'aSb%d' % par; dres = 'dg%d' % par
                    if fc == 0:
                        memset(aSf[:, 0:2], 0.0, [ares], eng='pool')
                    else:
                        cp(aSf[:, 0:2], halo[:, f, :], ['halo'], [ares], eng='pool')
                    for j in range(3):
                        ts(dg[:, par * 3 + j, :], ident[:], cw[:, f, j:j + 1], None, ALU.mult, None, ['ident', 'cw'], [dres])
                    for sc in range(FC // 512):
                        ts_ = slice(sc * 512, (sc + 1) * 512)
                        ba = rotbank(FFB); bg = rotbank(FFB); bc = rotbank(FFB)
                        gl_ = Pf[it_n[0] % 2]; gres = 'Pf%d' % (it_n[0] % 2); it_n[0] += 1
                        for k in range(DC):
                            mm(BK[ba], wa[:, k, fl * 128:(fl + 1) * 128], hTc[:, k, ts_], k == 0, k == DC - 1, ['ws%d' % ka_, 'hTc'], [ba])
                        for k in range(DC):
                            mm(BK[bg], wg[:, k, fl * 128:(fl + 1) * 128], hTc[:, k, ts_], k == 0, k == DC - 1, ['ws%d' % kg_, 'hTc'], [bg])
                        act(aSf[:, 2 + sc * 512:2 + (sc + 1) * 512], BK[ba], AF.Copy, [ba], [ares])
                        for j in range(3):
                            mm(BK[bc], dg[:, par * 3 + j, :], aSf[:, sc * 512 + j:sc * 512 + j + 512], j == 0, j == 2, [dres, ares], [bc])
                        act(gl_[:], BK[bc], AF.Gelu_apprx_tanh, [bc, 'cb'], [gres], bias=cb[:, f:f + 1])
                        tt(uT[:, f, ts_], gl_[:], BK[bg], ALU.mult, [gres, bg], ['uT'])
                    if fc < NFC - 1:
                        cp(halo[:, f, :], aSf[:, FC:FC + 2], [ares], ['halo'], eng='pool')
            return load, compute

        def ffn_out_piece(fc, hf, tg, f0, accb):
            nfp = min(8, NF - f0)
            st_ = {}
            def load():
                k = ws_next(); st_['k'] = k
                st_['w'] = WS[k][:, 0:nfp * 512].rearrange("p (f c) -> p f c", f=nfp)
                load_w(st_['w'], wfo_d[f0 * 128:(f0 + nfp) * 128, hf * 512:(hf + 1) * 512].rearrange("(f p) c -> p f c", p=128), 'ws%d' % k)
            def compute():
                k, wv2 = st_['k'], st_['w']
                for fl in range(nfp):
                    f = f0 + fl
                    for t4 in range(4):
                        tl = tg * 4 + t4
                        mm(BK[accb[t4]], uT[:, f, tl * 128:(tl + 1) * 128], wv2[:, fl, :], f == 0, f == NF - 1, ['uT', 'ws%d' % k], [accb[t4]])
                if f0 + nfp >= NF:
                    for t4 in range(4):
                        t = fc * ntl + tg * 4 + t4
                        k2 = tile_rr[0] % 2; tile_rr[0] += 1
                        hres = 'out_%d_%d' % (sq, t)
                        dma_in(xst[k2][:, 0:512], out_d[sq, t * 128:(t + 1) * 128, hf * 512:(hf + 1) * 512], ['xst%d' % k2], r=[hres])
                        tt(xst[k2][:, 0:512], xst[k2][:, 0:512], BK[accb[t4]], ALU.add, ['xst%d' % k2, accb[t4]], ['xst%d' % k2])
                        dma_out(out_d[sq, t * 128:(t + 1) * 128, hf * 512:(hf + 1) * 512], xst[k2][:, 0:512], ['xst%d' % k2], w=[hres + ('_f%d' % hf)])
            return load, compute

        def marker(fn):
            return (lambda: None), fn

        pieces = []
        for fc in range(NFC):
            pieces.append(marker(lambda: P.alias(['xnb0', 'xnb1'], ['aSb0', 'aSb1'])))
            for p0 in range(0, NF, 4):
                pieces.append(ffn_in_piece(fc, p0))
            pieces.append(marker(lambda: P.alias(['aSb0', 'aSb1'], ['xnb0', 'xnb1'])))
            nxt = list(range((fc + 1) * ntl, (fc + 2) * ntl)) if fc + 1 < NFC else []
            for hf in range(2):
                for tg in range(ntl // 4):
                    gsel = grp_n[0] % 2
                    accb = ACCS[gsel]; grp_n[0] += 1
                    f0 = 0
                    while f0 < NF:
                        ld, cmpt = ffn_out_piece(fc, hf, tg, f0, accb)
                        if gsel == 0 and nxt:
                            take = nxt[:2]; nxt = nxt[2:]
                            def wrapped(cmpt=cmpt, take=take, fc=fc):
                                ks = []
                                for t in take:
                                    ks.append(norm_transpose(out_d[sq, t * 128:(t + 1) * 128, :], 'Gt', Gt, hTc, None, 'hTc',
                                                             extra_r=['out_%d_%d' % (sq, t)], split=True))
                                cmpt()
                                for t, k in zip(take, ks):
                                    tl = t - (fc + 1) * ntl
                                    norm_post(k, hTc, slice(tl * 128, (tl + 1) * 128), 'hTc')
                            pieces.append((ld, wrapped))
                        else:
                            pieces.append((ld, cmpt))
                        f0 += min(8, NF - f0)
            assert not nxt
        pieces[0][0]()
        for i in range(len(pieces)):
            if i + 1 < len(pieces): pieces[i + 1][0]()
            pieces[i][1]()
        P.alias(L3, L1)

    P.emit()
    st.close()
    return nc


def _prep_inputs(inp, S):
    f = np.float32
    g = lambda k: np.asarray(inp[k], dtype=f)
    w_in = g("w_in")[0]
    offs = np.cumsum([0, 512, 64, 64, 256, 64, 4, 512, 512, 512])
    qa, ka, va, qi, ki, wi_, qb, kb, vb = [w_in[:, offs[i]:offs[i + 1]] for i in range(9)]
    w_in_r = np.ascontiguousarray(np.concatenate([qa, ka, ka, va, wi_, qb, kb, qi, ki, ki, vb], axis=1))
    rep = lambda v: np.ascontiguousarray(np.broadcast_to(np.asarray(v, dtype=f).reshape(1, -1), (128, np.asarray(v).size)))
    gn = np.concatenate([np.tile(g("g_qa")[0], 8), np.tile(g("g_ka")[0], 2), np.tile(g("g_qb")[0], 8), np.tile(g("g_kb")[0], 8)])
    lamv = np.concatenate([g("lam_q1")[0], g("lam_k1")[0], g("lam_q2")[0], g("lam_k2")[0]])
    invf = (1.0 / (10000.0 ** (np.arange(0, 64, 2, dtype=np.float32) / 64.0))).astype(f)
    cwv = g("conv_w")[0]
    convw = np.ascontiguousarray(cwv.reshape(3, NF, 128).transpose(2, 1, 0))
    convb = np.ascontiguousarray(g("conv_b")[0].reshape(NF, 128).T)
    shared = {
        "invf": rep(invf), "gmix": rep(g("g_mix")[0]), "gxat": rep(g("g_xattn")[0]), "gmem": rep(g("g_mem")[0]),
        "gffn": rep(g("g_ffn")[0]), "gn": rep(gn), "gxq": rep(np.tile(g("g_xq")[0], 4)), "gxk": rep(np.tile(g("g_xk")[0], 4)),
        "lamv": rep(lamv), "w_in": w_in_r, "w_out": g("w_out")[0], "w_xq": g("w_xq")[0], "w_xk": g("w_xk")[0],
        "w_xv": g("w_xv")[0], "w_xo": g("w_xo")[0], "w_ffn_in": g("w_ffn_in")[0], "w_ffn_out": g("w_ffn_out")[0],
        "convw": convw, "convb": convb,
    }
    return shared


_NC_CACHE = {}


def kernel(**inputs):
    x = np.asarray(inputs["x"], dtype=np.float32)
    mem = np.asarray(inputs["mem"], dtype=np.float32)
    pos = np.asarray(inputs["positions"]).astype(np.int32)
    B, S, _ = x.shape
    n_cores = 8
    n_seq = B // n_cores
    shared = _prep_inputs(inputs, S)
    key = (S, n_seq)
    if key not in _NC_CACHE:
        _NC_CACHE[key] = build(S, n_seq)
    nc = _NC_CACHE[key]
    in_maps = []
    for c in range(n_cores):
        sl = slice(c * n_seq, (c + 1) * n_seq)
        m = dict(shared)
        m["x"] = np.ascontiguousarray(x[sl]); m["mem"] = np.ascontiguousarray(mem[sl])
        m["pos"] = np.ascontiguousarray(pos[sl].reshape(n_seq, S // 128, 128).transpose(0, 2, 1))
        in_maps.append(m)
    res = run_bass_kernel_spmd(nc, in_maps, core_ids=list(range(n_cores)))
    out = np.concatenate([np.asarray(r["out"], dtype=np.float32) for r in res.results], axis=0)
    return out
```

```python
import numpy as np
from contextlib import ExitStack
import concourse.bass as bass
import concourse.mybir as mybir
from concourse.bass_utils import run_bass_kernel_spmd

F32 = mybir.dt.float32; BF16 = mybir.dt.bfloat16; I32 = mybir.dt.int32
AF = mybir.ActivationFunctionType
ALU = mybir.AluOpType
AX = mybir.AxisListType

D = 1024; DC = 8; NMEM = 256; DFF = 2816; NF = 22
NCOL = 2628
EPS = 1e-6
NBIS = 13
TWO_PI = 6.283185307179586


class _Op:
    __slots__ = ('eng', 'fn', 'deps', 'is_dma', 'dslot', 'dval', 'idx', 'signal', 'sigval')


class Prog:
    ENGS = ['pe', 'act', 'dve', 'pool', 'sp']

    def __init__(self, nc, n_dma=12):
        self.nc = nc
        self.ops = {e: [] for e in self.ENGS}
        self.last_w = {}
        self.readers = {}
        self.seen = {e: {} for e in self.ENGS}
        self.n_dma = n_dma
        self.dma_rr = {e: 0 for e in self.ENGS}
        self.dma_last = {}

    def op(self, eng, fn, reads=(), writes=(), dma=False):
        o = _Op(); o.eng = eng; o.fn = fn; o.is_dma = dma; o.signal = False; o.sigval = 0
        o.idx = len(self.ops[eng]); o.dslot = None; o.dval = 0
        deps = []
        for r in reads:
            w = self.last_w.get(r)
            if w is not None: deps.append(w)
        for r in writes:
            w = self.last_w.get(r)
            if w is not None: deps.append(w)
            rd = self.readers.get(r)
            if rd: deps.extend(rd.values())
        if dma:
            k = self.dma_rr[eng]; self.dma_rr[eng] = (k + 1) % self.n_dma
            o.dslot = (eng, k)
            prev = self.dma_last.get(o.dslot)
            if prev is not None:
                deps.append(prev); o.dval = prev.dval + 16
            else:
                o.dval = 16
            self.dma_last[o.dslot] = o
        seen = self.seen[eng]
        final = []
        for d in deps:
            if d.is_dma:
                key = d.dslot
                if seen.get(key, 0) >= d.dval: continue
                seen[key] = d.dval
                final.append(d)
            else:
                if d.eng == eng and eng not in ('act', 'pool', 'dve'): continue
                key = d.eng
                if seen.get(key, -1) >= d.idx: continue
                seen[key] = d.idx
                d.signal = True
                final.append(d)
        o.deps = final
        for r in reads:
            self.readers.setdefault(r, {})[o.dslot if dma else eng] = o
        for r in writes:
            self.last_w[r] = o
            self.readers[r] = {}
        self.ops[eng].append(o)
        return o

    def pe(self, fn, reads=(), writes=()): return self.op('pe', fn, reads, writes)
    def act(self, fn, reads=(), writes=()): return self.op('act', fn, reads, writes)
    def dve(self, fn, reads=(), writes=()): return self.op('dve', fn, reads, writes)
    def pool(self, fn, reads=(), writes=()): return self.op('pool', fn, reads, writes)
    def dma(self, fn, reads=(), writes=(), q='sp'): return self.op(q, fn, reads, writes, dma=True)

    def alias(self, old_names, new_names):
        best = {}
        def add(o):
            if o is None: return
            if o.is_dma: best[('d', o.dslot, o.dval)] = o
            else:
                p = best.get(('e', o.eng))
                if p is None or o.idx > p.idx: best[('e', o.eng)] = o
        for n in old_names:
            add(self.last_w.get(n))
            for r in self.readers.get(n, {}).values(): add(r)
        for n in old_names:
            self.last_w.pop(n, None); self.readers.pop(n, None)
        for n in new_names:
            self.last_w[n] = None
            self.readers[n] = dict(best)

    def emit(self):
        nc = self.nc
        for e in self.ENGS:
            c = 0
            for o in self.ops[e]:
                if o.signal and not o.is_dma:
                    c += 1; o.sigval = c
        with ExitStack() as st:
            esem = {e: st.enter_context(nc.semaphore("sem_" + e)) for e in self.ENGS}
            dsem = {}
            for slot in self.dma_last:
                dsem[slot] = st.enter_context(nc.semaphore("dsem_%s_%d" % slot))
            block = st.enter_context(nc.Block())

            def run(ename, e):
                for o in self.ops[ename]:
                    for d in o.deps:
                        if d.is_dma: e.wait_ge(dsem[d.dslot], d.dval)
                        else: e.wait_ge(esem[d.eng], d.sigval)
                    ins = o.fn(e)
                    if o.is_dma: ins.then_inc(dsem[o.dslot], 16)
                    elif o.signal: ins.then_inc(esem[ename], 1)
                if ename == 'sp':
                    for slot, o in self.dma_last.items():
                        e.wait_ge(dsem[slot], o.dval)

            @block.tensor
            def _(e): run('pe', e)
            @block.scalar
            def _(e): run('act', e)
            @block.vector
            def _(e): run('dve', e)
            @block.gpsimd
            def _(e): run('pool', e)
            @block.sync
            def _(e): run('sp', e)


def build(S, n_seq=2, topk=None, stop_after=None, dbg=False):
    NT = S // 128
    NQC = S // 512
    if topk is None: topk = min(256, S // 4)
    FC = min(1024, S)
    NFC = S // FC
    nc = bass.Bass("TRN2", target_bir_lowering=False)
    di = lambda n, s, d=F32: nc.dram_tensor(n, s, d, kind="ExternalInput").ap()
    x_d = di("x", [n_seq, S, D]); mem_d = di("mem", [n_seq, NMEM, D]); pos_d = di("pos", [n_seq, 128, NT], I32)
    invf_d = di("invf", [128, 32])
    gmix_d = di("gmix", [128, D]); gxat_d = di("gxat", [128, D]); gmem_d = di("gmem", [128, D]); gffn_d = di("gffn", [128, D])
    gn_d = di("gn", [128, 1664]); gxq_d = di("gxq", [128, D]); gxk_d = di("gxk", [128, D]); lam_d = di("lamv", [128, 256])
    win_d = di("w_in", [D, NCOL]); wout_d = di("w_out", [D, D]); wxq_d = di("w_xq", [D, D]); wxk_d = di("w_xk", [D, D])
    wxv_d = di("w_xv", [D, D]); wxo_d = di("w_xo", [D, D]); wfi_d = di("w_ffn_in", [D, 2 * DFF]); wfo_d = di("w_ffn_out", [DFF, D])
    cw_d = di("convw", [128, NF, 3]); cb_d = di("convb", [128, NF])
    out_d = nc.dram_tensor("out", [n_seq, S, D], F32, kind="ExternalOutput").ap()

    P = Prog(nc)
    st = ExitStack()
    _cnt = [0]
    def sb(shape, dt=F32, name=None):
        _cnt[0] += 1
        return st.enter_context(nc.sbuf_tensor("s_" + (name or ("t%d" % _cnt[0])), shape, dt))

    dbg_outs = {}
    def dump(name, ap, shape, dt, reads):
        if not dbg: return
        d = nc.dram_tensor("dbg_" + name, list(shape), dt, kind="ExternalOutput").ap()
        dbg_outs[name] = d
        P.dma(lambda e: e.dma_start(out=d, in_=ap), reads=reads, q='sp')

    Q = [st.enter_context(nc.psum_tensor("Q%d" % i, [128, 1024], F32)) for i in range(4)]
    BK = {}
    for i in range(4):
        BK['Q%da' % i] = Q[i][:, 0:512]; BK['Q%db' % i] = Q[i][:, 512:1024]
    QTv = {'Q3a': Q[3][:, 0:512].bitcast(BF16), 'Q3b': Q[3][:, 512:1024].bitcast(BF16), 'Q2a': Q[2][:, 0:512].bitcast(BF16)}
    rot = {}
    def rotbank(pool):
        k = rot.get(pool, 0); rot[pool] = k + 1
        return pool[k % len(pool)]
    MMB = ('Q0a', 'Q0b', 'Q1a', 'Q1b')
    TRB = ('Q3a', 'Q3b')

    NBIG = max(20 * S + 66 * NT, 8 * S + 3 * 2048, 22 * FC + 8 * FC)
    BIG = sb([128, NBIG], BF16, "BIG")
    ACT = sb([128, max(8 * S, 4 * S + 8192)], BF16, "ACTA")
    WS = [sb([128, 4096], BF16, "WS%d" % i) for i in range(4)]
    ws_rr = [0]
    def ws_next():
        k = ws_rr[0]; ws_rr[0] = (k + 1) % 4
        return k
    off = [0]
    def carve(n):
        a = BIG[:, off[0]:off[0] + n]; off[0] += n
        return a
    qaT = carve(4 * S).rearrange("p (c s) -> p c s", c=4)
    kaT = carve(S)
    qbT = carve(4 * S).rearrange("p (c s) -> p c s", c=4)
    kbT = carve(4 * S).rearrange("p (c s) -> p c s", c=4)
    qiT = carve(2 * S).rearrange("p (c s) -> p c s", c=2)
    kiT = carve(S)
    vb = carve(4 * S).rearrange("p (t c) -> p t c", t=NT)
    va = carve(66 * NT).rearrange("p (t c) -> p t c", t=NT)
    L1 = ['qaT', 'kaT', 'qbT', 'kbT', 'qiT', 'kiT', 'vb', 'va']
    qxT = BIG[:, 0:8 * S].rearrange("p (c s) -> p c s", c=8)
    kxT = BIG[:, 8 * S:8 * S + 2048].rearrange("p (c s) -> p c s", c=8)
    vx = BIG[:, 8 * S + 2048:8 * S + 4096].rearrange("p (t c) -> p t c", t=2)
    memT = BIG[:, 8 * S + 4096:8 * S + 6144].rearrange("p (c s) -> p c s", c=8)
    L2 = ['qxT', 'kxT', 'vx', 'memT']
    uT = BIG[:, 0:22 * FC].rearrange("p (f s) -> p f s", f=NF)
    hTc = BIG[:, 22 * FC:30 * FC].rearrange("p (c s) -> p c s", c=8)
    L3 = ['uT', 'hTc']
    xT = ACT[:, 0:8 * S].rearrange("p (c s) -> p c s", c=8)
    scoreF = ACT[:, 0:2 * S].bitcast(F32)
    maskq = ACT[:, 2 * S:3 * S]
    maskT = [ACT[:, 3 * S:4 * S].rearrange("p (k q) -> p k q", k=NT) for i in range(2)]
    relub = [ACT[:, 4 * S + i * 1024:4 * S + (i + 1) * 1024].bitcast(F32) for i in range(2)]
    mixA = ACT[0:64, 4 * S + 2048:4 * S + 6144].rearrange("p (h q) -> p h q", h=8)
    mixB = ACT[:, 4 * S + 6144:4 * S + 8192].rearrange("p (h q) -> p h q", h=4)
    LA = ['xT']
    LB = ['scoreF0', 'maskq0', 'maskT0', 'relub0', 'relub1', 'mixA', 'mixB']

    ident = sb([128, 128], BF16, "ident"); ones = sb([128, 128], BF16, "ones");
    negtri = sb([128, 128], F32, "negtri")
    pow2 = sb([128, NBIS], F32, "pow2"); mhalf = sb([128, 32], F32, "mhalf")
    invf = sb([128, 32], F32, "invf")
    Gt = sb([128, D], F32, "Gt"); gn = sb([128, 1664], F32, "gn")
    lamv = sb([128, 256], F32, "lamv"); lsc = sb([128, 8], F32, "lsc")
    cw = sb([128, NF, 3], F32, "cw"); cb = sb([128, NF], F32, "cb")
    posi = sb([128, NT], I32, "posi"); posf = sb([128, NT], F32, "posf")
    cosT = sb([128, NT, 32], F32, "cosT"); sinT = sb([128, NT, 32], F32, "sinT")
    rstdx = sb([128, NT], F32, "rstdx"); ssqx = sb([128, NT], F32, "ssqx")
    wi = sb([128, NT, 4], F32, "wi")
    xst = [sb([128, D], F32, "xst%d" % i) for i in range(2)]
    XNB = sb([128, 2 * D + 8], BF16, "XNB")
    xnb = [XNB[:, i * D:(i + 1) * D] for i in range(2)]
    sqj = sb([128, D], F32, "sqj")
    angt = xst[0][:, 0:NT * 32].rearrange("p (t c) -> p t c", c=32)
    angi = xst[1][:, 0:NT * 32].bitcast(I32).rearrange("p (t c) -> p t c", c=32)
    angk = sqj[:, 0:NT * 32].rearrange("p (t c) -> p t c", c=32)
    G2 = sb([128, D], F32, "G2")
    Pf = [G2[:, i * 512:(i + 1) * 512] for i in range(2)]
    Pn = sb([128, 512], F32, "Pn")
    Rb = [sb([128, 512], BF16, "Rb%d" % i) for i in range(2)]
    hs = sb([128, 64], F32, "hs")
    aS = sqj[:, 0:514]; cv = Pf[0]; gl = Pf[1]; t1 = Pn; t2 = Pf[0]; sqb = Rb[0]
    Eb = [sb([128, 1024], BF16, "Eb%d" % i) for i in range(2)]
    rs = sb([128, 1024], F32, "rs")
    rt = [rs[:, i * 256:(i + 1) * 256] for i in range(4)]
    bisS = [sb([128, 8], F32, "bis%d" % i) for i in range(2)]; wisS = [sb([128, NBIS], F32, "wis%d" % i) for i in range(2)]; wis2S = [sb([128, NBIS], F32, "wis2_%d" % i) for i in range(2)]
    halo = sb([128, NF, 2], F32, "halo")
    aSb = [XNB[:, i * (FC + 2):(i + 1) * (FC + 2)] for i in range(2)]
    dg = sb([128, 6, 128], BF16, "dg")
    epsT = sb([128, 1], F32, "epsT")
    onesf = sb([128, 128], F32, "onesf"); biasb = sb([128, 2, 512], BF16, "biasb"); tri01 = sb([128, 128], BF16, "tri01"); dbiasM = sb([128, 896], BF16, "dbiasM")

    def dma_in(out_ap, in_ap, w, q='sp', r=()):
        P.dma(lambda e: e.dma_start(out=out_ap, in_=in_ap), reads=r, writes=w, q=q)
    def dma_out(out_ap, in_ap, r, w=()):
        P.dma(lambda e: e.dma_start(out=out_ap, in_=in_ap), reads=r, writes=w, q='sp')
    def mm(out, lhsT, rhs, start, stop, r, w):
        P.pe(lambda e: e.matmul(out, lhsT=lhsT, rhs=rhs, start=start, stop=stop), reads=r, writes=w)
    def tr(out, in_, r, w):
        P.pe(lambda e: e.transpose(out=out, in_=in_, identity=ident[:]), reads=list(r) + ['ident'], writes=w)
    def act(out, in_, func, r, w, scale=None, bias=None, accum=None):
        kw = {}
        if scale is not None: kw['scale'] = scale
        if bias is not None: kw['bias'] = bias
        if accum is not None: kw['accum_out'] = accum
        P.act(lambda e: e.activation(out=out, in_=in_, func=func, **kw), reads=r, writes=w)
    def tt(out, in0, in1, op, r, w, eng='dve'):
        P.op(eng, lambda e: e.tensor_tensor(out=out, in0=in0, in1=in1, op=op), reads=r, writes=w)
    def ts(out, in0, s1, s2, op0, op1, r, w, accum=None, eng='dve'):
        if op1 is None:
            P.op(eng, lambda e: e.tensor_scalar(out=out, in0=in0, scalar1=s1, scalar2=None, op0=op0), reads=r, writes=w)
        elif accum is None:
            P.op(eng, lambda e: e.tensor_scalar(out=out, in0=in0, scalar1=s1, scalar2=s2, op0=op0, op1=op1), reads=r, writes=w)
        else:
            P.op(eng, lambda e: e.tensor_scalar(out=out, in0=in0, scalar1=s1, scalar2=s2, op0=op0, op1=op1, accum_out=accum), reads=r, writes=w)
    def stt(out, in0, scalar, in1, op0, op1, r, w):
        P.dve(lambda e: e.scalar_tensor_tensor(out=out, in0=in0, scalar=scalar, in1=in1, op0=op0, op1=op1), reads=r, writes=w)
    def red(out, in_, op, r, w):
        P.dve(lambda e: e.tensor_reduce(out=out, in_=in_, axis=AX.X, op=op), reads=r, writes=w)
    def cp(out, in_, r, w, eng='dve'):
        if eng == 'act':
            act(out, in_, AF.Copy, r, w)
        else:
            P.op(eng, lambda e: e.tensor_copy(out=out, in_=in_), reads=r, writes=w)
    def memset(ap, v, w, eng='dve'):
        P.op(eng, lambda e: e.memset(ap, v), writes=w)
    def recip(out, in_, r, w):
        P.dve(lambda e: e.reciprocal(out=out, in_=in_), reads=r, writes=w)
    def rstd_of(out, ssq, n, inv_dim, r, w):
        if n == 1:
            ts(out, ssq, inv_dim, EPS, ALU.mult, ALU.add, r, w)
            tt(out, out, mhalf[:, 0:n], ALU.pow, list(w) + ['mhalf'], w, eng='pool')
        else:
            act(out, ssq, AF.Ln, list(r) + ['epsT'], w, scale=inv_dim, bias=epsT[:, 0:1])
            act(out, out, AF.Exp, w, w, scale=-0.5)

    memset(epsT[:], EPS, ['epsT']); memset(onesf[:], 1.0, ['onesf']); memset(ones[:], 1.0, ['ones']); memset(mhalf[:], -0.5, ['mhalf'])
    identf = sqj[:, 0:128]
    memset(identf[:], 1.0, ['sqj'], eng='pool')
    P.pool(lambda e: e.affine_select(out=identf[:], in_=identf[:], pattern=[[-1, 128]], compare_op=ALU.is_equal,
                                     fill=0.0, base=0, channel_multiplier=1), reads=['sqj'], writes=['sqj'])
    cp(ident[:], identf[:], ['sqj'], ['ident'])
    memset(negtri[:], 0.0, ['negtri'], eng='pool')
    P.pool(lambda e: e.affine_select(out=negtri[:], in_=negtri[:], pattern=[[-1, 128]], compare_op=ALU.is_ge,
                                     fill=-1e30, base=0, channel_multiplier=1), reads=['negtri'], writes=['negtri'])
    memset(tri01[:], 1.0, ['tri01'], eng='pool')
    P.pool(lambda e: e.affine_select(out=tri01[:], in_=tri01[:], pattern=[[1, 128]], compare_op=ALU.is_ge,
                                     fill=0.0, base=0, channel_multiplier=-1), reads=['tri01'], writes=['tri01'])
    memset(dbiasM[:], 0.0, ['dbias'], eng='pool')
    P.pool(lambda e: e.affine_select(out=dbiasM[:], in_=dbiasM[:], pattern=[[1, 896]], compare_op=ALU.is_ge,
                                     fill=-30000.0, base=-384, channel_multiplier=-1), reads=['dbias'], writes=['dbias'])
    for i in range(NBIS):
        memset(pow2[:, i:i + 1], 2.0 ** -(i + 1), ['pow2'])
    dma_in(invf[:], invf_d[:, :], ['invf']); dma_in(gn[:], gn_d[:, :], ['gn']); dma_in(lamv[:], lam_d[:, :], ['lamv'])
    dma_in(cw[:], cw_d[:, :, :], ['cw']); dma_in(cb[:], cb_d[:, :], ['cb'])
    tt(sqj[:, 0:64], lamv[:, 0:64], lamv[:, 64:128], ALU.mult, ['lamv'], ['sqj'])
    tt(sqj[:, 64:128], lamv[:, 128:192], lamv[:, 192:256], ALU.mult, ['lamv'], ['sqj'])
    red(lsc[:, 0:2], sqj[:, 0:128].rearrange("p (a b) -> p a b", a=2), ALU.add, ['sqj'], ['lsc'])
    act(lsc[:, 2:4], lsc[:, 0:2], AF.Exp, ['lsc'], ['lsc'])
    tt(lsc[:, 4:5], lsc[:, 3:4], lsc[:, 2:3], ALU.subtract, ['lsc'], ['lsc'])
    ts(lsc[:, 5:6], lsc[:, 4:5], -0.2, None, ALU.add, None, ['lsc'], ['lsc'])
    neglam = lsc[:, 5:6]

    tile_rr = [0]
    def norm_transpose(src_rows_ap, Gname, Gtile, dstT, dst_cols, dst_res, ssq_out=None, rstd_out=None, extra_r=(), split=False):
        k = tile_rr[0] % 2; tile_rr[0] += 1
        xs_, xn_ = xst[k], xnb[k]
        dma_in(xs_[:], src_rows_ap, ['xst%d' % k], r=extra_r)
        ssq = ssq_out if ssq_out is not None else hs[:, 60:61]
        rstd = rstd_out if rstd_out is not None else hs[:, 61:62]
        act(sqj[:], xs_[:], AF.Square, ['xst%d' % k], ['sqj', 'st_ssq'], accum=ssq)
        rstd_of(rstd, ssq, 1, 1.0 / D, ['st_ssq'], ['st_rstd'])
        stt(xn_, xs_[:], rstd, Gtile[:], ALU.mult, ALU.mult, ['xst%d' % k, 'st_rstd', Gname], ['xnb%d' % k])
        if split:
            return k
        norm_post(k, dstT, dst_cols, dst_res)
        return k

    def norm_post(k, dstT, dst_cols, dst_res):
        xn_ = xnb[k]
        b = rotbank(TRB)
        for c in range(DC):
            tr(QTv[b][:, c * 128:(c + 1) * 128], xn_[:, c * 128:(c + 1) * 128], ['xnb%d' % k], [b])
        cp(dstT[:, :, dst_cols], QTv[b][:, :].rearrange("p (c s) -> p c s", c=8), [b], [dst_res], eng='act')

    def load_w(dst_ap, src_ap, res):
        dma_in(dst_ap, src_ap, [res], q='pool')

    for sq in range(n_seq):
        ws_rr[0] = 0
        P.alias(LB, LA)
        P.alias(['rsA', 'rsD'], ['rt0', 'rt1', 'rt2', 'rt3'])
        dma_in(Gt[:], gmix_d[:, :], ['Gt'])
        dma_in(posi[:], pos_d[sq, :, :], ['posi'])
        cp(posf[:], posi[:], ['posi'], ['posf'])
        tt(angt, posf[:].unsqueeze(2).broadcast_to([128, NT, 32]), invf[:].unsqueeze(1).broadcast_to([128, NT, 32]),
           ALU.mult, ['posf', 'invf'], ['xst0'])
        for which, dst in ((0, sinT), (1, cosT)):
            ts(angk, angt, 1.0 / TWO_PI, 0.5 + 0.25 * which, ALU.mult, ALU.add, ['xst0'], ['sqj'])
            cp(angi, angk, ['sqj'], ['xst1'])
            cp(angk, angi, ['xst1'], ['sqj'])
            stt(angk, angk, -6.28125, angt, ALU.mult, ALU.add, ['sqj', 'xst0'], ['sqj'])
            cp(dst[:], angi, ['xst1'], ['rope_tmp'])
            stt(angk, dst[:], -(TWO_PI - 6.28125), angk, ALU.mult, ALU.add, ['rope_tmp', 'sqj'], ['sqj'])
            if which:
                ts(angk, angk, np.pi / 2, None, ALU.add, None, ['sqj'], ['sqj'])
            ts(dst[:], angk, np.pi, -TWO_PI, ALU.is_gt, ALU.mult, ['sqj'], ['rope_tmp'])
            tt(angk, angk, dst[:], ALU.add, ['sqj', 'rope_tmp'], ['sqj'])
            ts(dst[:], angk, -np.pi, TWO_PI, ALU.is_lt, ALU.mult, ['sqj'], ['rope_tmp'])
            tt(angk, angk, dst[:], ALU.add, ['sqj', 'rope_tmp'], ['sqj'])
            act(dst[:], angk, AF.Sin, ['sqj', 'rope_tmp'], ['sinT' if which == 0 else 'cosT'])
        for t in range(NT):
            norm_transpose(x_d[sq, t * 128:(t + 1) * 128, :], 'Gt', Gt, xT, slice(t * 128, (t + 1) * 128), 'xT',
                           ssq_out=ssqx[:, t:t + 1], rstd_out=rstdx[:, t:t + 1])
        memset(va[:, :, 64:65], 1.0, ['va'])

        groups = [(0, 512, 'qa'), (512, 196, 'ka'), (708, 512, 'qb'), (1220, 512, 'kb'), (1732, 384, 'qi'), (2116, 512, 'vb')]
        gn_off = {'qa': 0, 'ka': 512, 'qb': 640, 'kb': 1152}
        gw = {}
        def load_group(gi):
            c0, ncg, kind = groups[gi]
            wsl = ws_next(); wres = 'ws%d' % wsl
            wv = WS[wsl][:, 0:8 * ncg].rearrange("p (k c) -> p k c", k=8)
            load_w(wv, win_d.rearrange("(k p) c -> p k c", p=128)[:, :, c0:c0 + ncg], wres)
            gw[gi] = (wv, wres)

        def p1_mm(gi, t):
            c0, ncg, kind = groups[gi]
            wv, wres = gw[gi]
            tok = slice(t * 128, (t + 1) * 128)
            b = rotbank(MMB); ps = BK[b][:, 0:ncg]
            for k in range(DC):
                mm(ps, xT[:, k, tok], wv[:, k, :], k == 0, k == DC - 1, ['xT', wres], [b])
            return b

        def p1_post(gi, t, b):
            c0, ncg, kind = groups[gi]
            tok = slice(t * 128, (t + 1) * 128)
            ps = BK[b][:, 0:ncg]
            if kind == 'vb':
                act(vb[:, t, :], ps, AF.Copy, [b], ['vb'])
                return
            nrope = {'qa': 512, 'ka': 128, 'qb': 512, 'kb': 512, 'qi': 384}[kind]
            nh = nrope // 64
            pf = Pf[t % 2]; pres = 'Pf%d' % (t % 2)
            act(pf[:, 0:ncg], ps, AF.Copy, [b], [pres])
            src = pf
            if kind == 'ka':
                cp(va[:, t, 0:64], pf[:, 128:192], [pres], ['va'], eng='pool')
                ts(wi[:, t, :], pf[:, 192:196], 1.0 / 16.0, None, ALU.mult, None, [pres], ['wi'], eng='pool')
            if kind != 'qi':
                hsl = hs[:, (t % 2) * 8:(t % 2) * 8 + nh]; hres = 'hs%d' % (t % 2)
                act(sqj[:, 0:nrope], BK[b][:, 0:nrope], AF.Square, [b], ['sqj'])
                red(hsl, sqj[:, 0:nrope].rearrange("p (h d) -> p h d", d=64), ALU.add, ['sqj'], [hres])
                g0 = gn_off[kind]
                tt(Pn[:, 0:nrope], pf[:, 0:nrope], gn[:, g0:g0 + nrope], ALU.mult, [pres, 'gn'], ['Pn'])
                act(hsl, hsl, AF.Ln, [hres, 'epsT'], [hres], scale=1.0 / 64, bias=epsT[:, 0:1])
                act(hsl, hsl, AF.Exp, [hres], [hres], scale=-0.5)
                tt(Pn[:, 0:nrope].rearrange("p (h d) -> p h d", d=64), Pn[:, 0:nrope].rearrange("p (h d) -> p h d", d=64),
                   hsl.unsqueeze(2).broadcast_to([128, nh, 64]), ALU.mult, ['Pn', hres], ['Pn'])
                src = Pn; pres = 'Pn'
            sv = src[:, 0:nrope].rearrange("p (h d) -> p h d", d=64)
            x1 = sv[:, :, 0:32]; x2 = sv[:, :, 32:64]
            cb_ = cosT[:, t, :].unsqueeze(1).broadcast_to([128, nh, 32])
            sb_ = sinT[:, t, :].unsqueeze(1).broadcast_to([128, nh, 32])
            rv = [rt[i][:, 0:nh * 32].rearrange("p (h d) -> p h d", d=32) for i in range(4)]
            R = Rb[t % 2]; rres = 'Rb%d' % (t % 2)
            Rv = R[:, 0:nrope].rearrange("p (h d) -> p h d", d=64)
            tt(rv[0], x1, cb_, ALU.mult, [pres, 'cosT'], ['rt0'])
            tt(rv[2], x2, cb_, ALU.mult, [pres, 'cosT'], ['rt2'], eng='pool')
            tt(rv[1], x2, sb_, ALU.mult, [pres, 'sinT'], ['rt1'])
            tt(rv[3], x1, sb_, ALU.mult, [pres, 'sinT'], ['rt3'], eng='pool')
            tt(Rv[:, :, 0:32], rv[0], rv[1], ALU.subtract, ['rt0', 'rt1'], [rres + 'lo'])
            tt(Rv[:, :, 32:64], rv[2], rv[3], ALU.add, ['rt2', 'rt3'], [rres + 'hi'], eng='pool')
            nb = nrope // 128
            tb = rotbank(TRB)
            for j in range(nb):
                tr(QTv[tb][:, j * 128:(j + 1) * 128], R[:, j * 128:(j + 1) * 128], [rres + 'lo', rres + 'hi'], [tb])
            tv = QTv[tb][:, 0:nb * 128].rearrange("p (c s) -> p c s", c=nb)
            if kind == 'qa': cp(qaT[:, :, tok], tv, [tb], ['qaT'], eng='act')
            elif kind == 'ka': cp(kaT[:, tok], QTv[tb][:, 0:128], [tb], ['kaT'], eng='act')
            elif kind == 'qb': cp(qbT[:, :, tok], tv, [tb], ['qbT'], eng='act')
            elif kind == 'kb': cp(kbT[:, :, tok], tv, [tb], ['kbT'], eng='act')
            else:
                cp(qiT[:, :, tok], tv[:, 0:2, :], [tb], ['qiT'], eng='act')
                cp(kiT[:, tok], QTv[tb][:, 256:384], [tb], ['kiT'], eng='act')

        load_group(0)
        pend = None
        for gi in range(len(groups)):
            if gi + 1 < len(groups): load_group(gi + 1)
            for t in range(NT):
                b = p1_mm(gi, t)
                if pend is not None: p1_post(*pend)
                pend = (gi, t, b)
        p1_post(*pend)

        if sq == 0:
            dump("xT", xT, [128, 8, S], BF16, ['xT']); dump("cosT", cosT[:], [128, NT, 32], F32, ['cosT']); dump("sinT", sinT[:], [128, NT, 32], F32, ['sinT'])
            dump("qaT", qaT, [128, 4, S], BF16, ['qaT']); dump("kaT", kaT, [128, S], BF16, ['kaT']); dump("qbT", qbT, [128, 4, S], BF16, ['qbT'])
            dump("kbT", kbT, [128, 4, S], BF16, ['kbT']); dump("qiT", qiT, [128, 2, S], BF16, ['qiT']); dump("kiT", kiT, [128, S], BF16, ['kiT'])
            dump("vb", vb, [128, NT, 512], BF16, ['vb']); dump("va", va, [128, NT, 66], BF16, ['va']); dump("wi", wi[:], [128, NT, 4], F32, ['wi'])
            dump("rstdx", rstdx[:], [128, NT], F32, ['st_rstd'])
            dump("hs", hs[:], [128, 64], F32, ['hs0', 'hs1']); dump("sqj", sqj[:], [128, D], F32, ['sqj']); dump("Pn", Pn[:], [128, 512], F32, ['Pn'])
            dump("Pf1", Pf[(NT - 1) % 2][:], [128, 512], F32, ['Pf%d' % ((NT - 1) % 2)])
        if stop_after == 'P1':
            break
        P.alias(LA, LB)
        P.alias(['rt0', 'rt1', 'rt2', 'rt3'], ['rsA', 'rsD'])
        P.alias(['ws1'], ['scoreF1']); P.alias(['xnb0', 'xnb1'], ['maskq1']); P.alias(['sqj'], ['sqjlo', 'ED1'])
        scoreFs = [scoreF, WS[1][:, :].bitcast(F32)]; maskqs = [maskq, XNB[:, 0:S]]
        woA = [None, None]; woAr = [None, None]
        for hf in range(2):
            k = ws_next(); woAr[hf] = 'ws%d' % k
            woA[hf] = WS[k][0:64, :].rearrange("p (h c) -> p h c", h=8)
            load_w(woA[hf], wout_d[0:512, hf * 512:(hf + 1) * 512].rearrange("(h d) c -> d h c", d=64), woAr[hf])
        k = ws_next(); woBr = 'ws%d' % k
        woB = WS[k][:, :].rearrange("p (h c) -> p h c", h=4)
        load_w(woB, wout_d[512:1024, :].rearrange("(h p) c -> p h c", p=128), woBr)

        mixA4 = mixA.rearrange("p (h two) q -> p h two q", two=2)
        rsA = rs[0:64, 0:512]; rsD = rs[:, 512:1024]
        rrow = rs[64:65, 0:512]

        def gen_bisect(qt):
            k_ = qt % 2
            scF = scoreFs[k_]; sres = 'scoreF%d' % k_; mq = maskqs[k_]; mqres = 'maskq%d' % k_
            bis = bisS[k_]; bres = 'bis%d' % k_; wis = wisS[k_]; wis2 = wis2S[k_]; wres = 'wis%d' % k_
            lo = bis[:, 1:2]; mid = bis[:, 3:4]; cnt = bis[:, 4:5]; tmp = bis[:, 5:6]; thr = bis[:, 6:7]
            qtok = slice(qt * 128, (qt + 1) * 128)
            nk = (qt + 1) * 128
            nkc = (nk + 511) // 512
            for kc in range(nkc):
                k0 = kc * 512; kn = min(512, nk - k0)
                for hp2 in range(2):
                    for half in range(2):
                        hp = slice(half * 64, half * 64 + 64); lb = ('Q2a', 'Q2b')[half]
                        mm(BK[lb][:, 0:kn], qiT[hp, hp2, qtok], kiT[hp, k0:k0 + kn], True, True, ['qiT', 'kiT'], [lb])
                    for half in range(2):
                        h = 2 * hp2 + half; lb = ('Q2a', 'Q2b')[half]
                        rb_ = relub[half]; rr_ = 'relub%d' % half
                        act(rb_[:, 0:kn], BK[lb][:, 0:kn], AF.Relu, [lb], [rr_])
                        if h == 0:
                            ts(scF[:, k0:k0 + kn], rb_[:, 0:kn], wi[:, qt, 0:1], None, ALU.mult, None, [rr_, 'wi'], [sres])
                        else:
                            stt(scF[:, k0:k0 + kn], rb_[:, 0:kn], wi[:, qt, h:h + 1], scF[:, k0:k0 + kn], ALU.mult, ALU.add,
                                [rr_, 'wi', sres], [sres])
                    yield
            red(bis[:, 0:1], scF[:, 0:nk], ALU.max, [sres], [bres])
            red(bis[:, 1:2], scF[:, 0:nk], ALU.min, [sres], [bres])
            yield
            tt(scF[:, nk - 128:nk], scF[:, nk - 128:nk], negtri[:], ALU.add, [sres, 'negtri'], [sres])
            tt(bis[:, 2:3], bis[:, 0:1], bis[:, 1:2], ALU.subtract, [bres], [bres])
            ts(wis[:], pow2[:], bis[:, 2:3], None, ALU.mult, None, ['pow2', bres], [wres])
            ts(wis2[:], pow2[:], bis[:, 2:3], 2.0, ALU.mult, ALU.mult, ['pow2', bres], [wres])
            tt(mid, lo, wis[:, 0:1], ALU.add, [bres, wres], [bres])
            yield
            for i in range(NBIS):
                ts(mq[:, 0:nk], scF[:, 0:nk], mid, None, ALU.is_ge, ALU.add, [sres, bres], [mqres, bres], accum=cnt)
                if i < NBIS - 1:
                    ts(tmp, cnt, topk - 0.5, wis2[:, i + 1:i + 2], ALU.is_ge, ALU.mult, [bres, wres], [bres])
                    stt(mid, mid, wis[:, i + 1:i + 2], tmp, ALU.subtract, ALU.add, [bres, wres], [bres])
                else:
                    ts(tmp, cnt, topk - 0.5, wis[:, i:i + 1], ALU.is_ge, ALU.mult, [bres, wres], [bres])
                    stt(thr, mid, wis[:, i:i + 1], tmp, ALU.subtract, ALU.add, [bres, wres], [bres])
                yield
            ts(mq[:, 0:nk], scF[:, 0:nk], thr, None, ALU.is_ge, None, [sres, bres], [mqres])
            yield

        def mask_transposes(qt):
            k_ = qt % 2
            mq = maskqs[k_]; mqres = 'maskq%d' % k_
            mT = maskT[0]
            for g0 in range(0, qt + 1, 8):
                g1 = min(qt + 1, g0 + 8)
                for kb in range(g0, g1):
                    tr(QTv['Q2a'][:, (kb - g0) * 128:(kb - g0 + 1) * 128], mq[:, kb * 128:(kb + 1) * 128], [mqres], ['Q2a'])
                cp(mT[:, g0:g1, :], QTv['Q2a'][:, 0:(g1 - g0) * 128].rearrange("p (k q) -> p k q", q=128), ['Q2a'], ['maskT0'], eng='act')
            if dbg and sq == 0 and qt == 1:
                dump("scoreF", scoreFs[k_][:, 0:S], [128, S], F32, ['scoreF%d' % k_]); dump("bis", bisS[k_][:], [128, 8], F32, ['bis%d' % k_])
                dump("maskq", mq, [128, S], BF16, [mqres]); dump("maskT", mT, [128, NT, 128], BF16, ['maskT0'])

        def gen_attn(qt):
            qtok = slice(qt * 128, (qt + 1) * 128)
            qq = qt % 4
            pure = (qt + 1) * 128 <= topk
            mT = maskT[0]
            def stage1(kb):
                ks = slice(kb * 128, (kb + 1) * 128)
                mm(BK['Q0a'], kaT[0:64, ks], qaT[0:64, :, qtok], True, True, ['kaT', 'qaT'], ['Q0a'])
                mm(BK['Q0b'], kaT[64:128, ks], qaT[64:128, :, qtok], True, True, ['kaT', 'qaT'], ['Q0b'])
                E = Eb[kb % 2]; eres = 'Eb%d' % (kb % 2)
                act(E[:], Q[0][:, :], AF.Exp, ['Q0a', 'Q0b'], [eres], scale=0.125)
                if not (pure and kb < qt):
                    if pure:
                        mk = tri01[:].unsqueeze(1).broadcast_to([128, 8, 128]); mr = 'tri01'
                    else:
                        mk = mT[:, kb, :].unsqueeze(1).broadcast_to([128, 8, 128]); mr = 'maskT0'
                    tt(E[:].rearrange("p (h q) -> p h q", h=8), E[:].rearrange("p (h q) -> p h q", h=8), mk, ALU.mult, [eres, mr], [eres])
            def stage2(kb):
                E = Eb[kb % 2]; eres = 'Eb%d' % (kb % 2)
                mm(BK['Q1a'][0:65, :], va[:, kb, 0:65], E[:, 0:512], kb == 0, kb == qt, ['va', eres], ['Q1a'])
                mm(BK['Q1b'][0:65, :], va[:, kb, 0:65], E[:, 512:1024], kb == 0, kb == qt, ['va', eres], ['Q1b'])
            prev = None
            for kb in range(qt + 1):
                stage1(kb)
                if prev is not None: stage2(prev)
                prev = kb
                yield
            stage2(prev)
            yield
            for par in range(2):
                ob = ('Q1a', 'Q1b')[par]; fb = ('Q0a', 'Q0b')[par]
                act(rrow, BK[ob][64:65, :], AF.Ln, [ob], ['rrow'])
                act(rrow, rrow, AF.Exp, ['rrow'], ['rrow'], scale=-1.0)
                P.pe(lambda e, fb=fb: e.matmul(BK[fb][0:64, :], lhsT=onesf[64:65, 0:64], rhs=rrow, start=True, stop=True),
                     reads=['onesf', 'rrow'], writes=[fb])
                cp(rsA, BK[fb][0:64, :], [fb], ['rsA'], eng='act')
                tt(mixA4[:, :, par, qq * 128:(qq + 1) * 128], BK[ob][0:64, :].rearrange("p (h q) -> p h q", h=4),
                   rsA.rearrange("p (h q) -> p h q", h=4), ALU.mult, [ob, 'rsA'], ['mixA'])
                yield

        def gen_diff(qc):
            qs = slice(qc * 512, (qc + 1) * 512)
            nkb = 4 * qc + 4
            A = [Pf[1], sqj[:, 0:512]]; Ares = ['Pf1', 'sqjlo']; Aeng = ['dve', 'pool']
            EDb = [biasb[:].rearrange("p a b -> p (a b)"), sqj[:, 512:1024].bitcast(BF16)]
            for h in range(4):
                def stage1(kb):
                    ks = slice(kb * 128, (kb + 1) * 128)
                    diag = kb >= 4 * qc
                    for c in range(2):
                        cp_ = slice(c * 64, c * 64 + 64); sb_ = ('Q2a', 'Q2b')[c]
                        mm(BK[sb_], kbT[cp_, h, ks], qbT[cp_, h, qs], True, not diag, ['kbT', 'qbT'], [sb_])
                    if diag:
                        for c in range(2):
                            sb_ = ('Q2a', 'Q2b')[c]
                            mm(BK[sb_], ident[:], dbiasM[:, 384 - 128 * (kb - 4 * qc):896 - 128 * (kb - 4 * qc)], False, True, ['ident', 'dbias'], [sb_])
                    act(EDb[kb % 2], Q[2][:, :], AF.Exp, ['Q2a', 'Q2b'], ['ED%d' % (kb % 2)], scale=0.125)
                def stage2(kb):
                    E = EDb[kb % 2]; eres = 'ED%d' % (kb % 2)
                    for c in range(2):
                        ob = ('Q3a', 'Q3b')[c]
                        Ec = E[:, c * 512:(c + 1) * 512]
                        mm(BK[ob], vb[:, kb, h * 128:(h + 1) * 128], Ec, kb == 0, kb == nkb - 1, ['vb', eres], [ob])
                        if kb == 0: cp(A[c], Ec, [eres], [Ares[c]], eng=Aeng[c])
                        else: tt(A[c], A[c], Ec, ALU.add, [eres, Ares[c]], [Ares[c]], eng=Aeng[c])
                prev = None
                for kb in range(nkb):
                    stage1(kb)
                    if prev is not None: stage2(prev)
                    prev = kb
                    yield
                stage2(prev)
                yield
                for c in range(2):
                    zb = ('Q2a', 'Q2b')[c]; ob = ('Q3a', 'Q3b')[c]
                    mm(BK[zb], onesf[:, :], A[c], True, True, ['onesf', Ares[c]], [zb])
                    act(rsD, BK[zb], AF.Ln, [zb], ['rsD'])
                    act(rsD, rsD, AF.Exp, ['rsD'], ['rsD'], scale=-1.0)
                    tt((t1 if c == 0 else t2)[:, :], BK[ob], rsD, ALU.mult, [ob, 'rsD'], ['Pn' if c == 0 else 'Pf0'])
                    yield
                stt(t1[:], t2[:], neglam, t1[:], ALU.mult, ALU.add, ['Pn', 'Pf0', 'lsc'], ['Pn'])
                tt(sqb[:], t1[:], t1[:], ALU.mult, ['Pn'], ['Rb0'])
                mm(BK['Q2a'], ones[:], sqb[:], True, True, ['ones', 'Rb0'], ['Q2a'])
                act(rsD, BK['Q2a'], AF.Ln, ['Q2a', 'epsT'], ['rsD'], scale=1.0 / 128, bias=epsT[:, 0:1])
                act(rsD, rsD, AF.Exp, ['rsD'], ['rsD'], scale=-0.5)
                stt(mixB[:, h, :], t1[:], 0.8, rsD, ALU.mult, ALU.mult, ['Pn', 'rsD'], ['mixB'])
                yield

        def emit_wout(qc):
            if dbg and sq == 0 and qc == 0:
                dump("mixA", mixA, [64, 8, 512], BF16, ['mixA']); dump("mixB", mixB, [128, 4, 512], BF16, ['mixB'])
            kq = []
            def _ld(tt_):
                t = qc * 4 + tt_
                k2 = tile_rr[0] % 2; tile_rr[0] += 1
                dma_in(xst[k2][:], x_d[sq, t * 128:(t + 1) * 128, :], ['xst%d' % k2])
                kq.append(k2)
            _ld(0)
            for tt_ in range(4):
                t = qc * 4 + tt_
                ml = slice(tt_ * 128, (tt_ + 1) * 128)
                if tt_ + 1 < 4: _ld(tt_ + 1)
                k2 = kq[tt_]
                for hf in range(2):
                    b = rotbank(('Q0a', 'Q0b'))
                    for hh in range(8):
                        mm(BK[b], mixA[:, hh, ml], woA[hf][:, hh, :], hh == 0, False, ['mixA', woAr[hf]], [b])
                    for hh in range(4):
                        mm(BK[b], mixB[:, hh, ml], woB[:, hh, hf * 512:(hf + 1) * 512], False, hh == 3, ['mixB', woBr], [b])
                    tt(xst[k2][:, hf * 512:(hf + 1) * 512], xst[k2][:, hf * 512:(hf + 1) * 512], BK[b], ALU.add, ['xst%d' % k2, b], ['xst%d' % k2])
                dma_out(out_d[sq, t * 128:(t + 1) * 128, :], xst[k2][:], ['xst%d' % k2], w=['out_%d_%d' % (sq, t)])

        def run_rr(fg, bg=()):
            fg = [g for g in fg if g is not None]; bg = [g for g in bg if g is not None]
            while fg:
                for g in list(fg):
                    try: next(g)
                    except StopIteration: fg.remove(g)
                for g in list(bg):
                    try: next(g)
                    except StopIteration: bg.remove(g)

        dgen = None
        older = None
        def bisected(t): return t < NT and (t + 1) * 128 > topk
        for s_ in range(NT + 1):
            fg = []
            if older is not None: fg.append(older)
            if s_ >= 1: fg.append(gen_attn(s_ - 1))
            younger = gen_bisect(s_ + 1) if bisected(s_ + 1) else None
            if s_ >= 1 and (s_ - 1) % 4 == 0:
                dgen = gen_diff((s_ - 1) // 4)
            last_of_chunk = s_ >= 1 and (s_ - 1) % 4 == 3
            if last_of_chunk:
                fg.append(dgen); run_rr(fg, [younger])
            else:
                run_rr(fg, [dgen, younger])
            if bisected(s_): mask_transposes(s_)
            if last_of_chunk:
                emit_wout((s_ - 1) // 4); dgen = None
            older = younger

        if stop_after == 'P2':
            continue
        P.alias(L1, L2 + ['hTc'])
        P.alias(['scoreF1'], ['ws1']); P.alias(['maskq1'], ['xnb0', 'xnb1']); P.alias(['sqjlo', 'ED1'], ['sqj'])
        P.alias(['Pf0', 'Pf1'], ['G2'])
        dma_in(Gt[:], gmem_d[:, :], ['Gt'])
        for mt in range(2):
            norm_transpose(mem_d[sq, mt * 128:(mt + 1) * 128, :], 'Gt', Gt, memT, slice(mt * 128, (mt + 1) * 128), 'memT')
        dma_in(G2[:], gxk_d[:, :], ['G2'])
        def wpieces(w_d):
            res = []
            for hf in range(2):
                k = ws_next()
                v = WS[k][:, :].rearrange("p (k c) -> p k c", k=8)
                load_w(v, w_d.rearrange("(k p) c -> p k c", p=128)[:, :, hf * 512:(hf + 1) * 512], 'ws%d' % k)
                res.append((v, 'ws%d' % k))
            return res

        def proj_headnorm(srcT, src_res, tok, wp, Gname, Gtile, dstT, dst_res, dst_cols):
            k2 = tile_rr[0] % 2; tile_rr[0] += 1
            xs_ = xst[k2]; xr = 'xst%d' % k2
            for hf in range(2):
                b = rotbank(MMB)
                for k in range(DC):
                    mm(BK[b], srcT[:, k, tok], wp[hf][0][:, k, :], k == 0, k == DC - 1, [src_res, wp[hf][1]], [b])
                for hh in range(2):
                    h4 = hf * 2 + hh
                    act(xs_[:, h4 * 256:(h4 + 1) * 256], BK[b][:, hh * 256:(hh + 1) * 256], AF.Copy, [b], [xr])
                    act(sqj[:, h4 * 256:(h4 + 1) * 256], BK[b][:, hh * 256:(hh + 1) * 256], AF.Square, [b], ['sqj', 'hs'], accum=hs[:, 32 + h4:33 + h4])
            rstd_of(hs[:, 32:36], hs[:, 32:36], 4, 1.0 / 256, ['hs'], ['hs'])
            for h4 in range(4):
                stt(xnb[k2][:, h4 * 256:(h4 + 1) * 256], xs_[:, h4 * 256:(h4 + 1) * 256], hs[:, 32 + h4:33 + h4], Gtile[:, h4 * 256:(h4 + 1) * 256],
                    ALU.mult, ALU.mult, [xr, 'hs', Gname], ['xnb%d' % k2])
            tb = rotbank(TRB)
            for c in range(DC):
                tr(QTv[tb][:, c * 128:(c + 1) * 128], xnb[k2][:, c * 128:(c + 1) * 128], ['xnb%d' % k2], [tb])
            cp(dstT[:, :, dst_cols], QTv[tb][:, :].rearrange("p (c s) -> p c s", c=8), [tb], [dst_res], eng='act')

        wk = wpieces(wxk_d)
        for mt in range(2):
            proj_headnorm(memT, 'memT', slice(mt * 128, (mt + 1) * 128), wk, 'G2', G2, kxT, 'kxT', slice(mt * 128, (mt + 1) * 128))
        wv_ = wpieces(wxv_d)
        for mt in range(2):
            for hf in range(2):
                b = rotbank(MMB)
                for k in range(DC):
                    mm(BK[b], memT[:, k, mt * 128:(mt + 1) * 128], wv_[hf][0][:, k, :], k == 0, k == DC - 1, ['memT', wv_[hf][1]], [b])
                cp(vx[:, mt, hf * 512:(hf + 1) * 512], BK[b], [b], ['vx'], eng='act')
        dma_in(Gt[:], gxat_d[:, :], ['Gt'])
        dma_in(G2[:], gxq_d[:, :], ['G2'])
        wq = wpieces(wxq_d)
        P.alias(LB, LA)
        def q_stageB(t):
            tok = slice(t * 128, (t + 1) * 128)
            hsl = hs[:, 32 + 4 * (t % 2):36 + 4 * (t % 2)]; hres = 'hsq%d' % (t % 2)
            banks = []
            for hf in range(2):
                b = rotbank(MMB); banks.append(b)
                for k in range(DC):
                    mm(BK[b], xT[:, k, tok], wq[hf][0][:, k, :], k == 0, k == DC - 1, ['xT', wq[hf][1]], [b])
                for hh in range(2):
                    h4 = hf * 2 + hh
                    act(sqj[:, h4 * 256:(h4 + 1) * 256], BK[b][:, hh * 256:(hh + 1) * 256], AF.Square, [b], ['sqj', hres], accum=hsl[:, h4:h4 + 1])
            return (t, banks)
        def q_stageC(t, banks):
            tok = slice(t * 128, (t + 1) * 128)
            k2 = tile_rr[0] % 2; tile_rr[0] += 1
            hsl = hs[:, 32 + 4 * (t % 2):36 + 4 * (t % 2)]; hres = 'hsq%d' % (t % 2)
            act(hsl, hsl, AF.Ln, [hres, 'epsT'], [hres], scale=1.0 / 256, bias=epsT[:, 0:1])
            act(hsl, hsl, AF.Exp, [hres], [hres], scale=-0.5)
            for h4 in range(4):
                b = banks[h4 // 2]; hh = h4 % 2
                stt(xnb[k2][:, h4 * 256:(h4 + 1) * 256], BK[b][:, hh * 256:(hh + 1) * 256], hsl[:, h4:h4 + 1], G2[:, h4 * 256:(h4 + 1) * 256],
                    ALU.mult, ALU.mult, [b, hres, 'G2'], ['xnb%d' % k2])
            tb = rotbank(TRB)
            for c in range(DC):
                tr(QTv[tb][:, c * 128:(c + 1) * 128], xnb[k2][:, c * 128:(c + 1) * 128], ['xnb%d' % k2], [tb])
            cp(qxT[:, :, tok], QTv[tb][:, :].rearrange("p (c s) -> p c s", c=8), [tb], ['qxT'], eng='act')
        pB = None
        for t in range(NT + 2):
            if t < NT:
                norm_transpose(out_d[sq, t * 128:(t + 1) * 128, :], 'Gt', Gt, xT, slice(t * 128, (t + 1) * 128), 'xT', extra_r=['out_%d_%d' % (sq, t)])
            pC = pB
            pB = q_stageB(t - 1) if 1 <= t <= NT else None
            if pC is not None: q_stageC(*pC)

        wo = wpieces(wxo_d)
        P.alias(LA, ['mixX'])
        dma_in(Gt[:], gffn_d[:, :], ['Gt'])
        mixX = xT
        def x_head_mm(qc, h):
            qs = slice(qc * 512, (qc + 1) * 512)
            hp_ = h % 2
            obs = (('Q1a', 'Q1b'), ('Q2a', 'Q2b'))[hp_]; zb = ('Q3a', 'Q3b')[hp_]
            for mt in range(2):
                b = ('Q0a', 'Q0b')[mt]
                for hh in range(2):
                    mm(BK[b], kxT[:, 2 * h + hh, mt * 128:(mt + 1) * 128], qxT[:, 2 * h + hh, qs], hh == 0, hh == 1, ['kxT', 'qxT'], [b])
                E = Eb[mt][:, 0:512]; eres = 'Eb%d' % mt
                act(E, BK[b], AF.Exp, [b], [eres], scale=1.0 / 16)
                for hh in range(2):
                    mm(BK[obs[hh]], vx[:, mt, (2 * h + hh) * 128:(2 * h + hh + 1) * 128], E, mt == 0, mt == 1, ['vx', eres], [obs[hh]])
                mm(BK[zb], ones[:], E, mt == 0, mt == 1, ['ones', eres], [zb])
        def x_head_fin(qc, h):
            qs = slice(qc * 512, (qc + 1) * 512)
            hp_ = h % 2
            obs = (('Q1a', 'Q1b'), ('Q2a', 'Q2b'))[hp_]; zb = ('Q3a', 'Q3b')[hp_]
            rz = rs[:, hp_ * 512:(hp_ + 1) * 512]; rzr = ('rsA', 'rsD')[hp_]
            act(rz, BK[zb], AF.Ln, [zb], [rzr])
            act(rz, rz, AF.Exp, [rzr], [rzr], scale=-1.0)
            for hh in range(2):
                tt(mixX[:, 2 * h + hh, qs], BK[obs[hh]], rz, ALU.mult, [obs[hh], rzr], ['mixX'])
        xwo_k = {}
        def x_wo_ld(t):
            k2 = tile_rr[0] % 2; tile_rr[0] += 1
            dma_in(xst[k2][:], out_d[sq, t * 128:(t + 1) * 128, :], ['xst%d' % k2], r=['out_%d_%d' % (sq, t)])
            xwo_k[t] = k2
        def x_wo(t):
            tok = slice(t * 128, (t + 1) * 128)
            k2 = xwo_k[t]
            for hf in range(2):
                b = rotbank(('Q0a', 'Q0b'))
                for k in range(DC):
                    mm(BK[b], mixX[:, k, tok], wo[hf][0][:, k, :], k == 0, k == DC - 1, ['mixX', wo[hf][1]], [b])
                tt(xst[k2][:, hf * 512:(hf + 1) * 512], xst[k2][:, hf * 512:(hf + 1) * 512], BK[b], ALU.add, ['xst%d' % k2, b], ['xst%d' % k2])
            dma_out(out_d[sq, t * 128:(t + 1) * 128, :], xst[k2][:], ['xst%d' % k2], w=['out_%d_%d' % (sq, t)])
        for qc in range(NQC):
            prevh = None
            for h in range(4):
                x_head_mm(qc, h)
                if prevh is not None: x_head_fin(qc, prevh)
                prevh = h
            x_head_fin(qc, prevh)
            for tt_ in range(4):
                t = qc * 4 + tt_
                x_wo_ld(t)
                x_wo(t)
                if t < FC // 128:
                    norm_transpose(out_d[sq, t * 128:(t + 1) * 128, :], 'Gt', Gt, hTc, slice(t * 128, (t + 1) * 128), 'hTc',
                                   extra_r=['out_%d_%d' % (sq, t)])

        if stop_after == 'P4':
            continue
        P.alias(L2, ['uT'])
        P.alias(['G2'], ['Pf0', 'Pf1'])
        P.alias(['mixX'], LB)
        memset(halo[:], 0.0, ['halo'])
        ntl = FC // 128
        FFB = ('Q0a', 'Q0b', 'Q1a', 'Q1b', 'Q2a', 'Q2b')
        ACCS = (('Q0a', 'Q0b', 'Q1a', 'Q1b'), ('Q2a', 'Q2b', 'Q3a', 'Q3b'))
        it_n = [0]; grp_n = [0]
        wsrc = wfi_d.rearrange("(k p) c -> p k c", p=128)

        def ffn_in_piece(fc, p0):
            npf = min(4, NF - p0)
            st_ = {}
            def load():
                ka_ = ws_next(); kg_ = ws_next()
                st_['ka'] = ka_; st_['kg'] = kg_
                st_['wa'] = WS[ka_][:, 0:8 * npf * 128].rearrange("p (k c) -> p k c", k=8)
                st_['wg'] = WS[kg_][:, 0:8 * npf * 128].rearrange("p (k c) -> p k c", k=8)
                load_w(st_['wa'], wsrc[:, :, p0 * 128:(p0 + npf) * 128], 'ws%d' % ka_)
                load_w(st_['wg'], wsrc[:, :, DFF + p0 * 128:DFF + (p0 + npf) * 128], 'ws%d' % kg_)
            def compute():
                ka_, kg_, wa, wg = st_['ka'], st_['kg'], st_['wa'], st_['wg']
                for fl in range(npf):
                    f = p0 + fl
                    par = f % 2
                    aSf = aSb[par]; ares = 'aSb%d' % par; dres = 'dg%d' % par
                    if fc == 0:
                        memset(aSf[:, 0:2], 0.0, [ares], eng='pool')
                    else:
                        cp(aSf[:, 0:2], halo[:, f, :], ['halo'], [ares], eng='pool')
                    for j in range(3):
                        ts(dg[:, par * 3 + j, :], ident[:], cw[:, f, j:j + 1], None, ALU.mult, None, ['ident', 'cw'], [dres])
                    for sc in range(FC // 512):
                        ts_ = slice(sc * 512, (sc + 1) * 512)
                        ba = rotbank(FFB); bg = rotbank(FFB); bc = rotbank(FFB)
                        gl_ = Pf[it_n[0] % 2]; gres = 'Pf%d' % (it_n[0] % 2); it_n[0] += 1
                        for k in range(DC):
                            mm(BK[ba], wa[:, k, fl * 128:(fl + 1) * 128], hTc[:, k, ts_], k == 0, k == DC - 1, ['ws%d' % ka_, 'hTc'], [ba])
                        for k in range(DC):
                            mm(BK[bg], wg[:, k, fl * 128:(fl + 1) * 128], hTc[:, k, ts_], k == 0, k == DC - 1, ['ws%d' % kg_, 'hTc'], [bg])
                        act(aSf[:, 2 + sc * 512:2 + (sc + 1) * 512], BK[ba], AF.Copy, [ba], [ares])
                        for j in range(3):
                            mm(BK[bc], dg[:, par * 3 + j, :], aSf[:, sc * 512 + j:sc * 512 + j + 512], j == 0, j == 2, [dres, ares], [bc])
                        act(gl_[:], BK[bc], AF.Gelu_apprx_tanh, [bc, 'cb'], [gres], bias=cb[:, f:f + 1])
                        tt(uT[:, f, ts_], gl_[:], BK[bg], ALU.mult, [gres, bg], ['uT'])
                    if fc < NFC - 1:
                        cp(halo[:, f, :], aSf[:, FC:FC + 2], [ares], ['halo'], eng='pool')
            return load, compute

        def ffn_out_piece(fc, hf, tg, f0, accb):
            nfp = min(8, NF - f0)
            st_ = {}
            def load():
                k = ws_next(); st_['k'] = k
                st_['w'] = WS[k][:, 0:nfp * 512].rearrange("p (f c) -> p f c", f=nfp)
                load_w(st_['w'], wfo_d[f0 * 128:(f0 + nfp) * 128, hf * 512:(hf + 1) * 512].rearrange("(f p) c -> p f c", p=128), 'ws%d' % k)
            def compute():
                k, wv2 = st_['k'], st_['w']
                for fl in range(nfp):
                    f = f0 + fl
                    for t4 in range(4):
                        tl = tg * 4 + t4
                        mm(BK[accb[t4]], uT[:, f, tl * 128:(tl + 1) * 128], wv2[:, fl, :], f == 0, f == NF - 1, ['uT', 'ws%d' % k], [accb[t4]])
                if f0 + nfp >= NF:
                    for t4 in range(4):
                        t = fc * ntl + tg * 4 + t4
                        k2 = tile_rr[0] % 2; tile_rr[0] += 1
                        hres = 'out_%d_%d' % (sq, t)
                        dma_in(xst[k2][:, 0:512], out_d[sq, t * 128:(t + 1) * 128, hf * 512:(hf + 1) * 512], ['xst%d' % k2], r=[hres])
                        tt(xst[k2][:, 0:512], xst[k2][:, 0:512], BK[accb[t4]], ALU.add, ['xst%d' % k2, accb[t4]], ['xst%d' % k2])
                        dma_out(out_d[sq, t * 128:(t + 1) * 128, hf * 512:(hf + 1) * 512], xst[k2][:, 0:512], ['xst%d' % k2], w=[hres + ('_f%d' % hf)])
            return load, compute

        def marker(fn):
            return (lambda: None), fn

        pieces = []
        for fc in range(NFC):
            pieces.append(marker(lambda: P.alias(['xnb0', 'xnb1'], ['aSb0', 'aSb1'])))
            for p0 in range(0, NF, 4):
                pieces.append(ffn_in_piece(fc, p0))
            pieces.append(marker(lambda: P.alias(['aSb0', 'aSb1'], ['xnb0', 'xnb1'])))
            nxt = list(range((fc + 1) * ntl, (fc + 2) * ntl)) if fc + 1 < NFC else []
            for hf in range(2):
                for tg in range(ntl // 4):
                    gsel = grp_n[0] % 2
                    accb = ACCS[gsel]; grp_n[0] += 1
                    f0 = 0
                    while f0 < NF:
                        ld, cmpt = ffn_out_piece(fc, hf, tg, f0, accb)
                        if gsel == 0 and nxt:
                            take = nxt[:2]; nxt = nxt[2:]
                            def wrapped(cmpt=cmpt, take=take, fc=fc):
                                ks = []
                                for t in take:
                                    ks.append(norm_transpose(out_d[sq, t * 128:(t + 1) * 128, :], 'Gt', Gt, hTc, None, 'hTc',
                                                             extra_r=['out_%d_%d' % (sq, t)], split=True))
                                cmpt()
                                for t, k in zip(take, ks):
                                    tl = t - (fc + 1) * ntl
                                    norm_post(k, hTc, slice(tl * 128, (tl + 1) * 128), 'hTc')
                            pieces.append((ld, wrapped))
                        else:
                            pieces.append((ld, cmpt))
                        f0 += min(8, NF - f0)
            assert not nxt
        pieces[0][0]()
        for i in range(len(pieces)):
            if i + 1 < len(pieces): pieces[i + 1][0]()
            pieces[i][1]()
        P.alias(L3, L1)

    P.emit()
    st.close()
    return nc


def _prep_inputs(inp, S):
    f = np.float32
    g = lambda k: np.asarray(inp[k], dtype=f)
    w_in = g("w_in")[0]
    offs = np.cumsum([0, 512, 64, 64, 256, 64, 4, 512, 512, 512])
    qa, ka, va, qi, ki, wi_, qb, kb, vb = [w_in[:, offs[i]:offs[i + 1]] for i in range(9)]
    w_in_r = np.ascontiguousarray(np.concatenate([qa, ka, ka, va, wi_, qb, kb, qi, ki, ki, vb], axis=1))
    rep = lambda v: np.ascontiguousarray(np.broadcast_to(np.asarray(v, dtype=f).reshape(1, -1), (128, np.asarray(v).size)))
    gn = np.concatenate([np.tile(g("g_qa")[0], 8), np.tile(g("g_ka")[0], 2), np.tile(g("g_qb")[0], 8), np.tile(g("g_kb")[0], 8)])
    lamv = np.concatenate([g("lam_q1")[0], g("lam_k1")[0], g("lam_q2")[0], g("lam_k2")[0]])
    invf = (1.0 / (10000.0 ** (np.arange(0, 64, 2, dtype=np.float32) / 64.0))).astype(f)
    cwv = g("conv_w")[0]
    convw = np.ascontiguousarray(cwv.reshape(3, NF, 128).transpose(2, 1, 0))
    convb = np.ascontiguousarray(g("conv_b")[0].reshape(NF, 128).T)
    shared = {
        "invf": rep(invf), "gmix": rep(g("g_mix")[0]), "gxat": rep(g("g_xattn")[0]), "gmem": rep(g("g_mem")[0]),
        "gffn": rep(g("g_ffn")[0]), "gn": rep(gn), "gxq": rep(np.tile(g("g_xq")[0], 4)), "gxk": rep(np.tile(g("g_xk")[0], 4)),
        "lamv": rep(lamv), "w_in": w_in_r, "w_out": g("w_out")[0], "w_xq": g("w_xq")[0], "w_xk": g("w_xk")[0],
        "w_xv": g("w_xv")[0], "w_xo": g("w_xo")[0], "w_ffn_in": g("w_ffn_in")[0], "w_ffn_out": g("w_ffn_out")[0],
        "convw": convw, "convb": convb,
    }
    return shared


_NC_CACHE = {}


def kernel(**inputs):
    x = np.asarray(inputs["x"], dtype=np.float32)
    mem = np.asarray(inputs["mem"], dtype=np.float32)
    pos = np.asarray(inputs["positions"]).astype(np.int32)
    B, S, _ = x.shape
    n_cores = 8
    n_seq = B // n_cores
    shared = _prep_inputs(inputs, S)
    key = (S, n_seq)
    if key not in _NC_CACHE:
        _NC_CACHE[key] = build(S, n_seq)
    nc = _NC_CACHE[key]
    in_maps = []
    for c in range(n_cores):
        sl = slice(c * n_seq, (c + 1) * n_seq)
        m = dict(shared)
        m["x"] = np.ascontiguousarray(x[sl]); m["mem"] = np.ascontiguousarray(mem[sl])
        m["pos"] = np.ascontiguousarray(pos[sl].reshape(n_seq, S // 128, 128).transpose(0, 2, 1))
        in_maps.append(m)
    res = run_bass_kernel_spmd(nc, in_maps, core_ids=list(range(n_cores)))
    out = np.concatenate([np.asarray(r["out"], dtype=np.float32) for r in res.results], axis=0)
    return out
```
